# Optimizing a Trainium2 kernel written in Bass

```python
import math, functools
import jax, jax.numpy as jnp
from jax import lax
import numpy as np

D_MODEL = 1024
BATCH = 16
SEQ = 2048
DEPTH = 1
DEC_BATCH = 128
DEC_SEQ = 8
PAST_LEN = 16384
PAGE_SIZE = 128

MLA_HEADS = 8
MLA_NOPE = 64
MLA_ROPE = 32
MLA_V = 64
MLA_Q_LORA = 384
MLA_KV_LORA = 256
MLA_SCALE = (MLA_NOPE + MLA_ROPE) ** -0.5
Q_BLOCK = 128
ROPE_THETA = 10000.0
GLA_HEADS = 4
GLA_DK = 64
GLA_DV = 128
GLA_GATE_RANK = 16
GLA_GATE_TAU = 16.0
GLA_CHUNK = 64
D_MIX = MLA_HEADS * MLA_V + GLA_HEADS * GLA_DV
D_FF = 2816
N_ADA = 9
EPS = 1e-6
IN_SIZES = (MLA_Q_LORA, MLA_KV_LORA, MLA_ROPE, GLA_HEADS * GLA_DK, GLA_HEADS * GLA_DK,
            GLA_HEADS * GLA_DV, GLA_GATE_RANK, GLA_HEADS * GLA_DV)
D_IN = sum(IN_SIZES)
IN_SPLIT_POINTS = tuple(sum(IN_SIZES[:i + 1]) for i in range(len(IN_SIZES) - 1))

kernel_name = 'hymba_mla_gla_macaron_adaln_step'

F32 = jnp.float32


def rmsnorm(x, g):
    xf = x.astype(F32)
    y = xf * lax.rsqrt(jnp.mean(xf * xf, axis=-1, keepdims=True) + EPS)
    return (y * g.astype(F32)).astype(x.dtype)


def rope(x, pos):
    R = x.shape[-1]
    inv = ROPE_THETA ** (-jnp.arange(0, R, 2, dtype=F32) / R)
    ang = pos[:, None] * inv[None, :]
    shape = (pos.shape[0],) + (1,) * (x.ndim - 3) + (R // 2,)
    cos = jnp.cos(ang).reshape(shape)
    sin = jnp.sin(ang).reshape(shape)
    xf = x.astype(F32)
    x1, x2 = xf[..., :R // 2], xf[..., R // 2:]
    return jnp.concatenate([x1 * cos - x2 * sin, x1 * sin + x2 * cos], axis=-1).astype(x.dtype)


def swiglu(h, w1, w3, w2):
    return (jax.nn.silu(h @ w1) * (h @ w3)) @ w2


def modulate(h, shift, scale):
    return h * (1.0 + scale) + shift


def mla_prompt_attend(q_lat, q_rope, ckv, krope):
    B, T, H, C = q_lat.shape
    nb = T // Q_BLOCK
    kpos = jnp.arange(T)
    neg = jnp.finfo(F32).min

    def block(args):
        i, ql, qr = args
        s = (jnp.einsum('bqhc,bsc->bhqs', ql, ckv, preferred_element_type=F32)
             + jnp.einsum('bqhr,bsr->bhqs', qr, krope, preferred_element_type=F32)) * MLA_SCALE
        qpos = i * Q_BLOCK + jnp.arange(Q_BLOCK)
        s = jnp.where(kpos[None, :] <= qpos[:, None], s, neg)
        p = jax.nn.softmax(s, axis=-1)
        return jnp.einsum('bhqs,bsc->bqhc', p.astype(ckv.dtype), ckv)

    qlb = q_lat.reshape(B, nb, Q_BLOCK, H, C).transpose(1, 0, 2, 3, 4)
    qrb = q_rope.reshape(B, nb, Q_BLOCK, H, -1).transpose(1, 0, 2, 3, 4)
    o = lax.map(block, (jnp.arange(nb), qlb, qrb))
    return o.transpose(1, 0, 2, 3, 4).reshape(B, T, H, C)


def mla_sample_attend(q_lat, q_rope, ckv, krope, ckv_past, krope_past):
    T = q_lat.shape[1]
    P = ckv_past.shape[1]
    s_past = (jnp.einsum('bthc,bpc->bhtp', q_lat, ckv_past, preferred_element_type=F32)
              + jnp.einsum('bthr,bpr->bhtp', q_rope, krope_past, preferred_element_type=F32)) * MLA_SCALE
    s_new = (jnp.einsum('bthc,bsc->bhts', q_lat, ckv, preferred_element_type=F32)
             + jnp.einsum('bthr,bsr->bhts', q_rope, krope, preferred_element_type=F32)) * MLA_SCALE
    causal = jnp.tril(jnp.ones((T, T), bool))
    s_new = jnp.where(causal, s_new, jnp.finfo(F32).min)
    p = jax.nn.softmax(jnp.concatenate([s_past, s_new], axis=-1), axis=-1).astype(ckv.dtype)
    return (jnp.einsum('bhtp,bpc->bthc', p[..., :P], ckv_past)
            + jnp.einsum('bhts,bsc->bthc', p[..., P:], ckv))


def gla_chunked(q, k, v, log_a, S0):
    B, T, H, DK = q.shape
    C = GLA_CHUNK if T % GLA_CHUNK == 0 else T
    n = T // C

    def to_chunks(a):
        return a.astype(F32).reshape(B, n, C, H, -1).transpose(1, 0, 3, 2, 4)

    mask = jnp.tril(jnp.ones((C, C), bool))

    def step(S, inp):
        qc, kc, vc, gc = inp
        b = jnp.cumsum(gc, axis=2)
        qe = qc * jnp.exp(b)
        ke = kc * jnp.exp(-b)
        A = jnp.where(mask, jnp.einsum('bhtk,bhsk->bhts', qe, ke), 0.0)
        o = jnp.einsum('bhtk,bhkv->bhtv', qe, S) + jnp.einsum('bhts,bhsv->bhtv', A, vc)
        b_last = b[:, :, -1:, :]
        S = (jnp.exp(b_last[:, :, 0, :])[..., None] * S
             + jnp.einsum('bhck,bhcv->bhkv', kc * jnp.exp(b_last - b), vc))
        return S, o

    S, o = lax.scan(step, S0.astype(F32), (to_chunks(q), to_chunks(k), to_chunks(v), to_chunks(log_a)))
    o = o.transpose(1, 0, 3, 2, 4).reshape(B, T, H, -1)
    return o, S


def mixer(h, pos, attend, gla_S0, w_in, g_qa, w_qb, g_kva, w_kvb, w_gate_b, b_gate, g_gla_o, w_o):
    B, T, _ = h.shape
    q_a, kv_a, k_r, gq, gk, gv, ga, gg = jnp.split(h @ w_in, IN_SPLIT_POINTS, axis=-1)
    q = jnp.einsum('btq,qhd->bthd', rmsnorm(q_a, g_qa), w_qb)
    q_rope = rope(q[..., MLA_NOPE:], pos)
    q_lat = jnp.einsum('bthn,chn->bthc', q[..., :MLA_NOPE], w_kvb[..., :MLA_NOPE])
    ckv = rmsnorm(kv_a, g_kva)
    krope = rope(k_r, pos)
    o_lat = attend(q_lat, q_rope, ckv, krope)
    o_mla = jnp.einsum('bthc,chv->bthv', o_lat, w_kvb[..., MLA_NOPE:]).reshape(B, T, MLA_HEADS * MLA_V)
    gq = gq.reshape(B, T, GLA_HEADS, GLA_DK) * (GLA_DK ** -0.5)
    gk = gk.reshape(B, T, GLA_HEADS, GLA_DK)
    gv = gv.reshape(B, T, GLA_HEADS, GLA_DV)
    log_a = (jax.nn.log_sigmoid((ga @ w_gate_b + b_gate).astype(F32)) / GLA_GATE_TAU).reshape(B, T, GLA_HEADS, GLA_DK)
    o_gla, S = gla_chunked(gq, gk, gv, log_a, gla_S0)
    o_gla = rmsnorm(o_gla.astype(h.dtype), g_gla_o) * jax.nn.silu(gg.reshape(B, T, GLA_HEADS, GLA_DV))
    out = jnp.concatenate([o_mla, o_gla.reshape(B, T, GLA_HEADS * GLA_DV)], axis=-1) @ w_o
    return out, ckv, krope, S.astype(h.dtype)


def layer(x, c, pos, attend, gla_S0, w_ada, b_ada, norm_ffn1, ffn1_w1, ffn1_w3, ffn1_w2, norm_mix,
          w_in, g_qa, w_qb, g_kva, w_kvb, w_gate_b, b_gate, g_gla_o, w_o,
          norm_ffn2, ffn2_w1, ffn2_w3, ffn2_w2):
    m = jax.nn.silu(c) @ w_ada + b_ada
    sh1, sc1, g1, sh2, sc2, g2, sh3, sc3, g3 = jnp.split(m[:, None, :], N_ADA, axis=-1)
    x = x + 0.5 * g1 * swiglu(modulate(rmsnorm(x, norm_ffn1), sh1, sc1), ffn1_w1, ffn1_w3, ffn1_w2)
    mix, ckv, krope, S = mixer(modulate(rmsnorm(x, norm_mix), sh2, sc2), pos, attend, gla_S0,
                               w_in, g_qa, w_qb, g_kva, w_kvb, w_gate_b, b_gate, g_gla_o, w_o)
    x = x + g2 * mix
    x = x + 0.5 * g3 * swiglu(modulate(rmsnorm(x, norm_ffn2), sh3, sc3), ffn2_w1, ffn2_w3, ffn2_w2)
    return x, ckv, krope, S


def setup_inputs(seed: int = 0) -> dict:
    key = jax.random.key(seed)
    ks = iter(jax.random.split(key, 48))

    def nrm(shape, s):
        return jax.random.normal(next(ks), shape, F32) * s

    def gain(n):
        return 1.0 + nrm((DEPTH, n), 0.05)

    n_pages = PAST_LEN // PAGE_SIZE
    n_phys = (DEC_BATCH * n_pages * 5) // 4
    page_table = jax.random.permutation(next(ks), n_phys)[:DEC_BATCH * n_pages].reshape(DEC_BATCH, n_pages).astype(jnp.int32)
    return {
        'x_prompt': nrm((BATCH, SEQ, D_MODEL), 1.0),
        'x_sample': nrm((DEC_BATCH, DEC_SEQ, D_MODEL), 1.0),
        'cache_ckv': nrm((DEPTH, n_phys, PAGE_SIZE, MLA_KV_LORA), 1.0),
        'cache_krope': nrm((DEPTH, n_phys, PAGE_SIZE, MLA_ROPE), 1.0),
        'state_gla': nrm((DEPTH, DEC_BATCH, GLA_HEADS, GLA_DK, GLA_DV), 0.3),
        'page_table': page_table,
        'c_prompt': nrm((BATCH, D_MODEL), 1.0),
        'c_sample': nrm((DEC_BATCH, D_MODEL), 1.0),
        'w_ada': nrm((DEPTH, D_MODEL, N_ADA * D_MODEL), 0.5 * D_MODEL ** -0.5),
        'b_ada': nrm((DEPTH, N_ADA * D_MODEL), 0.02),
        'norm_ffn1': gain(D_MODEL),
        'ffn1_w1': nrm((DEPTH, D_MODEL, D_FF), D_MODEL ** -0.5),
        'ffn1_w3': nrm((DEPTH, D_MODEL, D_FF), D_MODEL ** -0.5),
        'ffn1_w2': nrm((DEPTH, D_FF, D_MODEL), D_FF ** -0.5),
        'norm_mix': gain(D_MODEL),
        'w_in': nrm((DEPTH, D_MODEL, D_IN), D_MODEL ** -0.5),
        'g_qa': gain(MLA_Q_LORA),
        'w_qb': nrm((DEPTH, MLA_Q_LORA, MLA_HEADS, MLA_NOPE + MLA_ROPE), MLA_Q_LORA ** -0.5),
        'g_kva': gain(MLA_KV_LORA),
        'w_kvb': nrm((DEPTH, MLA_KV_LORA, MLA_HEADS, MLA_NOPE + MLA_V), MLA_KV_LORA ** -0.5),
        'w_gate_b': nrm((DEPTH, GLA_GATE_RANK, GLA_HEADS * GLA_DK), GLA_GATE_RANK ** -0.5),
        'b_gate': nrm((DEPTH, GLA_HEADS * GLA_DK), 0.1),
        'g_gla_o': gain(GLA_DV),
        'w_o': nrm((DEPTH, D_MIX, D_MODEL), D_MIX ** -0.5),
        'norm_ffn2': gain(D_MODEL),
        'ffn2_w1': nrm((DEPTH, D_MODEL, D_FF), D_MODEL ** -0.5),
        'ffn2_w3': nrm((DEPTH, D_MODEL, D_FF), D_MODEL ** -0.5),
        'ffn2_w2': nrm((DEPTH, D_FF, D_MODEL), D_FF ** -0.5),
        'norm_final': 1.0 + nrm((D_MODEL,), 0.05),
    }


def reference(x_prompt, x_sample, cache_ckv, cache_krope, state_gla, page_table, c_prompt, c_sample,
              w_ada, b_ada, norm_ffn1, ffn1_w1, ffn1_w3, ffn1_w2, norm_mix, w_in, g_qa, w_qb, g_kva, w_kvb,
              w_gate_b, b_gate, g_gla_o, w_o, norm_ffn2, ffn2_w1, ffn2_w3, ffn2_w2, norm_final):
    n_seq_s, n_pages = page_table.shape
    past_len = n_pages * PAGE_SIZE
    pos_p = jnp.arange(x_prompt.shape[1], dtype=F32)
    pos_s = past_len + jnp.arange(x_sample.shape[1], dtype=F32)
    hp, hs = x_prompt, x_sample
    ckv_p_l, kr_p_l, gla_p_l, ckv_s_l, kr_s_l, gla_s_l = [], [], [], [], [], []
    for l in range(DEPTH):
        lw = (w_ada[l], b_ada[l], norm_ffn1[l], ffn1_w1[l], ffn1_w3[l], ffn1_w2[l], norm_mix[l],
              w_in[l], g_qa[l], w_qb[l], g_kva[l], w_kvb[l], w_gate_b[l], b_gate[l], g_gla_o[l], w_o[l],
              norm_ffn2[l], ffn2_w1[l], ffn2_w3[l], ffn2_w2[l])
        ckv_past = cache_ckv[l, page_table].reshape(n_seq_s, past_len, MLA_KV_LORA)
        krope_past = cache_krope[l, page_table].reshape(n_seq_s, past_len, MLA_ROPE)
        attend_s = functools.partial(mla_sample_attend, ckv_past=ckv_past, krope_past=krope_past)
        S0_p = jnp.zeros((hp.shape[0], GLA_HEADS, GLA_DK, GLA_DV), hp.dtype)
        hp, ckv_p, kr_p, S_p = layer(hp, c_prompt, pos_p, mla_prompt_attend, S0_p, *lw)
        hs, ckv_s, kr_s, S_s = layer(hs, c_sample, pos_s, attend_s, state_gla[l], *lw)
        ckv_p_l.append(ckv_p); kr_p_l.append(kr_p); gla_p_l.append(S_p)
        ckv_s_l.append(ckv_s); kr_s_l.append(kr_s); gla_s_l.append(S_s)
    y_prompt = rmsnorm(hp, norm_final)
    y_sample = rmsnorm(hs, norm_final)
    return (y_prompt, y_sample, jnp.stack(ckv_p_l), jnp.stack(kr_p_l), jnp.stack(gla_p_l),
            jnp.stack(ckv_s_l), jnp.stack(kr_s_l), jnp.stack(gla_s_l))
```

```python
import contextlib
import numpy as np
import concourse.bass as bass
import concourse.mybir as mybir
from concourse.bass_utils import run_bass_kernel_spmd

F32 = mybir.dt.float32
BF16 = mybir.dt.bfloat16
I32 = mybir.dt.int32
ALU = mybir.AluOpType
AF = mybir.ActivationFunctionType
AX = mybir.AxisListType

N_DMA_SEMS = 24
LISTING = None
ENGS = ("pe", "act", "dve", "pool", "sp")
EPS = 1e-6
MLA_SCALE = 96.0 ** -0.5
ROPE_THETA = 10000.0
PAGE = 128


class Buf:
    __slots__ = ("name", "last_w", "readers")

    def __init__(self, name):
        self.name = name
        self.last_w = None
        self.readers = []


class V:
    __slots__ = ("bufs", "ap")

    def __init__(self, bufs, ap):
        self.bufs = bufs
        self.ap = ap


class T:
    def __init__(self, name, handle, bufs=None):
        self.name = name
        self.h = handle
        self.bufs = bufs or (Buf(name),)

    def __getitem__(self, idx):
        return V(self.bufs, self.h[idx])

    def v(self, ap):
        return V(self.bufs, ap)


class Instr:
    __slots__ = ("eng", "fn", "deps", "odeps", "signal", "is_dma", "dma_sem", "dma_val", "ticket",
                 "cost", "seq", "pos", "nbytes", "succ", "nun", "avail", "done")

    def __init__(self, eng, fn):
        self.eng = eng
        self.fn = fn
        self.deps = []
        self.odeps = []
        self.signal = False
        self.is_dma = False
        self.dma_sem = None
        self.dma_val = 0
        self.ticket = 0
        self.cost = 0.2
        self.seq = 0
        self.pos = 0
        self.nbytes = 0


class Prog:
    def __init__(self, nc):
        self.nc = nc
        self.streams = {e: [] for e in ENGS}
        self.stack = contextlib.ExitStack()
        self.dma_ring = {e: 0 for e in ENGS}
        self.dma_count = {}
        self.dma_last = {}
        self.segs = [{e: [] for e in ENGS}]
        self.tails = []
        self.nseq = 0

    def dram(self, name, shape, dtype, kind):
        h = self.nc.dram_tensor(name, list(shape), dtype, kind=kind)
        return T(name, h.ap())

    def _record(self, eng, fn, reads, writes, is_dma=False, cost=0.2, nbytes=0):
        ins = Instr(eng, fn)
        ins.is_dma = is_dma
        ins.cost = cost
        ins.nbytes = nbytes
        ins.seq = self.nseq
        self.nseq += 1
        seen = {}
        def add(d, kind):
            if d is ins:
                return
            k = id(d)
            if k in seen:
                if kind != "war":
                    seen[k][1] = kind
                return
            ent = [d, kind]
            seen[k] = ent
            ins.odeps.append(ent)
        for v in reads:
            for b in v.bufs:
                if b.last_w is not None:
                    add(b.last_w, "raw")
        for v in writes:
            for b in v.bufs:
                if b.last_w is not None:
                    add(b.last_w, "waw")
                for r in b.readers:
                    add(r, "war")
        if is_dma:
            slot = self.dma_ring[eng]
            self.dma_ring[eng] = (slot + 1) % N_DMA_SEMS
            key = (eng, slot)
            prev = self.dma_last.get(key)
            if prev is not None:
                add(prev, "raw")
            self.dma_count[key] = self.dma_count.get(key, 0) + 1
            ins.dma_sem = key
            ins.dma_val = 16 * self.dma_count[key]
            self.dma_last[key] = ins
        for v in reads:
            for b in v.bufs:
                b.readers.append(ins)
        for v in writes:
            for b in v.bufs:
                b.last_w = ins
                b.readers = []
        self.segs[-1][eng].append(ins)
        return ins

    @staticmethod
    def _free(ap):
        n = 1
        for d in ap.shape[1:]:
            n *= d
        return n

    def _ecost(self, eng, out):
        n = self._free(out.ap)
        if eng == "act":
            return 0.2 + n / 1200.0
        if eng == "dve":
            return 0.1 + n / 960.0
        return 0.2 + n / 500.0

    def op(self, eng, fn, reads=(), writes=(), cost=None):
        writes = list(writes)
        if cost is None:
            cost = self._ecost(eng, writes[0]) if writes else 0.2
        return self._record(eng, fn, list(reads), writes, cost=cost)

    def dma(self, eng, out, in_, **kw):
        nb = out.ap.shape[0] * self._free(out.ap) * _SZ.get(in_.ap.dtype, 4)
        return self._record(eng, lambda e: e.dma_start(out=out.ap, in_=in_.ap, **kw), [in_], [out], is_dma=True,
                            cost=(1.0 if eng == "pool" else 0.1), nbytes=nb)

    def idma(self, out, in_, idx, **kw):
        nb = out.ap.shape[0] * self._free(out.ap) * 4
        return self._record("pool", lambda e: e.indirect_dma_start(
            out=out.ap, out_offset=None, in_=in_.ap,
            in_offset=bass.IndirectOffsetOnAxis(ap=idx.ap, axis=0), **kw), [in_, idx], [out], is_dma=True,
            cost=1.5, nbytes=nb)

    def _pecost(self, lhsT, rhs):
        n = self._free(rhs.ap)
        m = self._free(lhsT.ap)
        c = (max(n, 48) / 1950.0 + m / 2800.0 + 0.02)
        if lhsT.ap.dtype == F32:
            c *= 4
        return c

    def matmul(self, out, lhsT, rhs, start=True, stop=True):
        reads = [lhsT, rhs] + ([] if start else [out])
        return self.op("pe", lambda e: e.matmul(out.ap, lhsT.ap, rhs.ap, start=start, stop=stop), reads, [out],
                       cost=self._pecost(lhsT, rhs))

    def transpose(self, out, in_, ident):
        return self.op("pe", lambda e: e.transpose(out.ap, in_.ap, ident.ap), [in_, ident], [out],
                       cost=self._pecost(in_, ident))

    def act(self, out, in_, func, bias=None, scale=None, accum_out=None):
        reads = [in_]
        kw = {}
        if bias is not None:
            if isinstance(bias, V):
                reads.append(bias)
                kw["bias"] = bias.ap
            else:
                kw["bias"] = bias
        if scale is not None:
            if isinstance(scale, V):
                reads.append(scale)
                kw["scale"] = scale.ap
            else:
                kw["scale"] = scale
        writes = [out]
        if accum_out is not None:
            writes.append(accum_out)
            kw["accum_out"] = accum_out.ap
        return self.op("act", lambda e: e.activation(out.ap, in_.ap, func, **kw), reads, writes)

    def tt(self, eng, out, in0, in1, op):
        return self.op(eng, lambda e: e.tensor_tensor(out.ap, in0.ap, in1.ap, op), [in0, in1], [out])

    def ts(self, eng, out, in0, s1, op0, s2=None, op1=None):
        reads = [in0]
        a1 = s1.ap if isinstance(s1, V) else s1
        a2 = s2.ap if isinstance(s2, V) else s2
        if isinstance(s1, V):
            reads.append(s1)
        if isinstance(s2, V):
            reads.append(s2)
        if op1 is None:
            return self.op(eng, lambda e: e.tensor_single_scalar(out.ap, in0.ap, a1, op0), reads, [out])
        return self.op(eng, lambda e: e.tensor_scalar(out.ap, in0.ap, a1, a2, op0, op1), reads, [out])

    def stt(self, eng, out, in0, scalar, in1, op0, op1):
        reads = [in0, in1]
        a = scalar.ap if isinstance(scalar, V) else scalar
        if isinstance(scalar, V):
            reads.append(scalar)
        return self.op(eng, lambda e: e.scalar_tensor_tensor(out.ap, in0.ap, a, in1.ap, op0, op1), reads, [out])

    def copy(self, eng, out, in_):
        if eng == "act":
            return self.op("act", lambda e: e.copy(out.ap, in_.ap), [in_], [out])
        return self.op(eng, lambda e: e.tensor_copy(out.ap, in_.ap), [in_], [out])

    def memset(self, eng, out, val):
        return self.op(eng, lambda e: e.memset(out.ap, val), [], [out])

    def recip(self, out, in_):
        return self.op("dve", lambda e: e.reciprocal(out.ap, in_.ap), [in_], [out])

    def barrier(self, markers):
        marks = {}
        for e in ("pe", "act", "dve", "pool"):
            ins = markers[e]()
            assert self.segs[-1][e][-1] is ins
            self.segs[-1][e].pop()
            marks[e] = ins
        outstanding = list(self.dma_last.values())
        tail = {}
        for e in ENGS:
            w = Instr(e, None)
            w.deps = [m for m in marks.values() if m.eng != e] + outstanding
            tail[e] = ([marks[e]] if e in marks else []) + [w]
        self.tails.append(tail)
        self.segs.append({e: [] for e in ENGS})

    def _schedule_segment(self, seg):
        import heapq
        allins = [i for e in ENGS for i in seg[e]]
        inseg = set(id(i) for i in allins)
        for i in allins:
            i.succ = []
            i.nun = 0
            i.avail = 0.0
            i.done = None
        for i in allins:
            for d, kind in i.odeps:
                if id(d) in inseg:
                    d.succ.append(i)
                    i.nun += 1
        heaps = {e: [] for e in ENGS}
        for i in allins:
            if i.nun == 0:
                heapq.heappush(heaps[i.eng], (0.0, i.seq, i))
        free = {e: 0.0 for e in ENGS}
        dma_free = [0.0]
        order = {e: [] for e in ENGS}
        remaining = len(allins)
        while remaining:
            best = None
            for e in ENGS:
                h = heaps[e]
                if not h:
                    continue
                cands = []
                while h and h[0][0] <= free[e] and len(cands) < 64:
                    cands.append(heapq.heappop(h))
                if cands:
                    cands.sort(key=lambda t: t[1])
                    pick = cands[0]
                    for c in cands[1:]:
                        heapq.heappush(h, c)
                    st = free[e]
                else:
                    pick = heapq.heappop(h)
                    st = pick[0]
                if best is None or st < best[0]:
                    if best is not None:
                        heapq.heappush(heaps[best[2].eng], best[1])
                    best = (st, pick, pick[2])
                else:
                    heapq.heappush(h, pick)
            st, pick, ins = best
            e = ins.eng
            if ins.is_dma:
                free[e] = st + ins.cost
                t0 = max(st + ins.cost, dma_free[0])
                dma_free[0] = t0 + ins.nbytes / 150e3
                ins.done = dma_free[0] + 2.0
            elif e == "pe":
                free[e] = st + ins.cost
                ins.done = st + ins.cost + 0.2
            else:
                free[e] = st + ins.cost
                ins.done = st + ins.cost + 0.05
            order[e].append(ins)
            remaining -= 1
            for sc in ins.succ:
                sc.nun -= 1
                if ins.done > sc.avail:
                    sc.avail = ins.done
                if sc.nun == 0:
                    heapq.heappush(heaps[sc.eng], (sc.avail + (0.1 if sc.eng != e else 0.0), sc.seq, sc))
        return order

    def _finalize(self, reorder=True):
        self.streams = {e: [] for e in ENGS}
        for k, seg in enumerate(self.segs):
            order = self._schedule_segment(seg) if reorder else seg
            for e in ENGS:
                self.streams[e].extend(order[e])
                if k < len(self.tails):
                    self.streams[e].extend(self.tails[k][e])
        for e in ENGS:
            for p, ins in enumerate(self.streams[e]):
                ins.pos = p
        for e in ENGS:
            for ins in self.streams[e]:
                if ins.fn is None:
                    for d in ins.deps:
                        d.signal = True
                    continue
                if not ins.odeps and not ins.deps:
                    continue
                best = {}
                deps = list(ins.deps)
                for d, kind in ins.odeps:
                    if d.is_dma:
                        deps.append(d)
                        continue
                    if d.eng == e and not ins.is_dma:
                        assert d.pos < ins.pos
                        if e == "pe" or kind == "war":
                            continue
                    b = best.get(d.eng)
                    if b is None or d.pos > b.pos:
                        best[d.eng] = d
                for d in best.values():
                    d.signal = True
                    deps.append(d)
                ins.deps = deps

    def emit(self, reorder=True):
        nc = self.nc
        self._finalize(reorder)
        for e in ENGS:
            t = 0
            for ins in self.streams[e]:
                if ins.signal and not ins.is_dma and ins.fn is not None:
                    t += 1
                    ins.ticket = t
        with contextlib.ExitStack() as st:
            esem = {e: st.enter_context(nc.semaphore("sem_" + e)) for e in ENGS}
            dsem = {}
            for key in self.dma_count:
                dsem[key] = st.enter_context(nc.semaphore("dsem_%s_%d" % key))
            block = st.enter_context(nc.Block())

            def target(d):
                if d.is_dma:
                    return dsem[d.dma_sem], d.dma_val
                return esem[d.eng], d.ticket

            def run(eng_name, eng):
                waited = {}
                for ins in self.streams[eng_name]:
                    for d in ins.deps:
                        sem, val = target(d)
                        k = id(sem)
                        if waited.get(k, 0) >= val:
                            continue
                        waited[k] = val
                        eng.wait_ge(sem, val)
                    if ins.fn is None:
                        continue
                    r = ins.fn(eng)
                    if LISTING is not None:
                        LISTING.append(str(getattr(r.ins, "name", "?")) + " " + r.concise())
                    if ins.is_dma:
                        r.then_inc(dsem[ins.dma_sem], 16)
                    elif ins.signal:
                        r.then_inc(esem[eng_name], 1)
                if eng_name == "sp":
                    for key, cnt in self.dma_count.items():
                        if waited.get(id(dsem[key]), 0) < 16 * cnt:
                            eng.wait_ge(dsem[key], 16 * cnt)

            @block.tensor
            def _(pe):
                run("pe", pe)

            @block.scalar
            def _(act):
                run("act", act)

            @block.vector
            def _(dve):
                run("dve", dve)

            @block.gpsimd
            def _(pool):
                run("pool", pool)

            @block.sync
            def _(sp):
                run("sp", sp)
        self.stack.close()


_SZ = {F32: 4, BF16: 2, I32: 4}


class Arena:
    def __init__(self, P, nf32):
        self.h = P.stack.enter_context(P.nc.sbuf_tensor("arena", [128, nf32], F32))
        self.n = nf32
        self.off = 0
        self.cnt = 0

    def alloc(self, name, shape, dtype):
        parts = shape[0]
        free = 1
        for s in shape[1:]:
            free *= s
        nbytes = free * _SZ[dtype]
        nf = (nbytes + 31) // 32 * 8
        assert self.off + nf <= self.n, "arena overflow at %s: need %d have %d" % (name, nf, self.n - self.off)
        ap = self.h[0:parts, self.off:self.off + nf]
        self.off += nf
        if dtype != F32:
            ap = ap.bitcast(dtype)
        ap = ap[:, 0:free]
        if len(shape) == 3:
            ap = ap.rearrange("p (a b) -> p a b", a=shape[1])
        elif len(shape) == 4:
            ap = ap.rearrange("p (a b c) -> p a b c", a=shape[1], b=shape[2])
        self.cnt += 1
        return T("%s_%d" % (name, self.cnt), ap)

    def mark(self):
        return self.off

    def reset(self, m):
        self.off = m


class PSBank:
    def __init__(self, ps, b0, nb):
        self.ps = ps
        self.b0 = b0
        self.nb = nb

    def _bufs(self, c0, c1):
        return tuple(self.ps.bufs[self.b0 + b] for b in range(c0 // 512, (c1 - 1) // 512 + 1))

    def f(self, c0, c1, p0=0, p1=128):
        base = self.b0 * 512
        return V(self._bufs(c0, c1), self.ps.h[p0:p1, base + c0:base + c1])

    def bf(self, c0, c1, p0=0, p1=128):
        base = self.b0 * 1024
        return V(self._bufs(c0 // 2, (c1 + 1) // 2), self.ps.hb[p0:p1, base + c0:base + c1])


class PSum:
    def __init__(self, P):
        self.h = P.stack.enter_context(P.nc.psum_tensor("psum_all", [128, 4096], F32))
        self.hb = self.h[:, :].bitcast(BF16)
        self.bufs = [Buf("psb%d" % i) for i in range(8)]
        self.cur = 0
        self.held = set()

    def get(self, nb=1, hold=False):
        c = self.cur
        for _ in range(16):
            if c + nb > 8:
                c = 0
            if all((c + i) not in self.held for i in range(nb)):
                break
            c = (c + 1) % 8
        else:
            raise RuntimeError("no free PSUM banks")
        b = PSBank(self, c, nb)
        self.cur = (c + nb) % 8
        if hold:
            for i in range(nb):
                self.held.add(c + i)
        return b

    def release(self, b):
        for i in range(b.nb):
            self.held.discard(b.b0 + i)


C_IDENT, C_CAUSAL, C_SBLK, C_OBLK8, C_TRIP, C_ONESP, C_SEG, C_CHK, C_ONE = 0, 128, 256, 384, 512, 640, 768, 784, 786
NCST = 800


def make_consts():
    c = np.zeros((128, NCST), np.float32)
    i = np.arange(128)
    s, q = i[:, None], i[None, :]
    c[:, C_IDENT:C_IDENT + 128] = np.eye(128)
    c[:, C_CAUSAL:C_CAUSAL + 128] = (s <= q)
    c[:, C_SBLK:C_SBLK + 128] = (s <= q) & (s // 8 == q // 8)
    c[:, C_OBLK8:C_OBLK8 + 128] = (s // 8 == q // 8)
    c[:, C_TRIP:C_TRIP + 128] = (s <= q) & (s // 64 == q // 64)
    c[:, C_ONESP:C_ONESP + 128] = (s // 64 == q // 64)
    c[:, C_SEG:C_SEG + 16] = (i[:, None] // 8 == np.arange(16)[None, :])
    c[:, C_CHK:C_CHK + 2] = (i[:, None] // 64 == np.arange(2)[None, :])
    c[:, C_ONE:C_ONE + 14] = 1.0
    return c


def rope_tables(seq, past):
    inv = (ROPE_THETA ** (-np.arange(0, 32, 2, dtype=np.float32) / np.float32(32))).astype(np.float32)
    pos_p = np.arange(seq, dtype=np.float32)
    pos_s = (past + np.arange(8)).astype(np.float32)
    pos_s_tile = np.tile(pos_s, 16)
    pos = np.concatenate([pos_p, pos_s_tile])
    ang = (pos[:, None] * inv[None, :]).astype(np.float32)
    cos = np.cos(ang).astype(np.float32)
    sin = np.sin(ang).astype(np.float32)
    nt = seq // 128 + 1
    tok = np.concatenate([cos, sin], axis=1).reshape(nt, 128, 32).transpose(1, 0, 2)
    ct = np.concatenate([cos, cos], axis=1).T
    sg = np.concatenate([-sin, sin], axis=1).T
    feat = np.stack([ct, sg], axis=1)
    return np.ascontiguousarray(tok, np.float32), np.ascontiguousarray(feat, np.float32)


def build_program(cfg):
    NPS, SEQ, NSS, NPG, NPHYS, DFF = cfg["NPS"], cfg["SEQ"], cfg["NSS"], cfg["NPG"], cfg["NPHYS"], cfg["DFF"]
    assert NSS == 16 and SEQ % 128 == 0 and DFF % 128 == 0 and NPG <= 128
    NT = SEQ // 128
    NF = DFF // 128
    NB = NPS + NSS
    TB = min(4, NT)
    NTP = NPS * NT
    NTT = NTP + 1
    TOKP = NPS * SEQ
    KP = NPG
    D = 1024

    nc = bass.Bass("TRN2", target_bir_lowering=False)
    P = Prog(nc)
    din = lambda n, s, d=F32: P.dram(n, s, d, "ExternalInput")
    dout = lambda n, s: P.dram(n, s, F32, "ExternalOutput")
    xp_d = din("xp", [TOKP, D]); xs_d = din("xs", [128, D])
    ckvc_d = din("cache_ckv", [NPHYS, PAGE * 256]); krc_d = din("cache_krope", [NPHYS, PAGE * 32])
    state_d = din("state", [NSS, 4, 64, 128])
    ptT_d = din("ptT", [NPG, NSS], I32)
    cT_d = din("cT", [D, NB])
    wada_d = din("w_ada", [D, 9 * D]); badaT_d = din("b_adaT", [128, 72])
    normsT_d = din("normsT", [128, 24])
    nfin_d = din("nfin_bc", [128, D])
    ffn_d = [[din("f%d_w1" % i, [D, DFF]), din("f%d_w3" % i, [D, DFF]), din("f%d_w2" % i, [DFF, D])] for i in (1, 2)]
    win_d = din("w_in", [D, 2224])
    gqab_d = din("g_qa_bc", [128, 384])
    wqbn_d = din("wqb_n", [384, 512]); wqbr_d = din("wqb_r", [384, 256]); wqbs_d = din("wqb_s", [384, 256])
    gkva_d = din("g_kva_bc", [128, 256])
    wkvbn_d = din("wkvb_nT", [64, 8 * 256]); wkvbv_d = din("wkvb_v", [256, 8 * 64])
    wgate_d = din("wgate", [17, 256])
    ggla_d = din("g_gla_bc", [128, 128])
    wo_d = din("w_o", [D, D])
    cst_d = din("consts", [128, NCST])
    ropeT_d = din("rope_tok", [128, NT + 1, 32]); ropeF_d = din("rope_feat", [32, 2, SEQ + 128])

    yp_d = dout("y_p", [TOKP, D]); ys_d = dout("y_s", [128, D])
    ckvp_d = dout("ckv_p", [TOKP, 256]); krp_d = dout("kr_p", [TOKP, 32]); glap_d = dout("gla_p", [NPS, 4, 64, 128])
    ckvs_d = dout("ckv_s", [128, 256]); krs_d = dout("kr_s", [128, 32]); glas_d = dout("gla_s", [NSS, 4, 64, 128])
    dbg = cfg.get("debug", False)
    phases = cfg.get("phases", ("ffn1", "mixer", "ffn2"))
    X1 = P.dram("x1_scr", [NTT * 128, D], F32, "ExternalOutput" if dbg else "Internal")
    X2 = P.dram("x2_scr", [NTT * 128, D], F32, "ExternalOutput" if dbg else "Internal")

    MIXD = P.dram("mix_dbg", [NTT * 128, D], F32, "ExternalOutput") if dbg else None
    AR = Arena(P, 53100)
    PS = PSum(P)

    cst = AR.alloc("cst", [128, NCST], F32)
    cstb = AR.alloc("cstb", [128, NCST], BF16)
    P.dma("sp", cst[:, :], cst_d[:, :])
    P.dma("pool", cstb[:, :], cst_d[:, :])
    ident_f = lambda n=128: cst[0:n, C_IDENT:C_IDENT + n]
    ident_b = lambda n=128: cstb[0:n, C_IDENT:C_IDENT + n]
    mT = AR.alloc("mT", [128, 72, NB], F32)
    Amod = AR.alloc("Amod", [128, 3, 8, NB], F32)
    normsT = AR.alloc("normsT", [128, 24], F32)
    badaT = AR.alloc("badaT", [128, 72], F32)
    mk = AR.alloc("mk", [128, 8], F32)
    P.dma("sp", normsT[:, :], normsT_d[:, :])
    P.dma("sp", badaT[:, :], badaT_d[:, :])

    def do_barrier():
        def m_pe():
            b = PS.get(1)
            return P.matmul(b.f(0, 1, 0, 1), cstb[0:1, C_ONE:C_ONE + 1], cstb[0:1, C_ONE:C_ONE + 1])
        P.barrier({
            "pe": m_pe,
            "act": lambda: P.copy("act", mk[0:1, 0:1], cst[0:1, C_ONE:C_ONE + 1]),
            "dve": lambda: P.memset("dve", mk[0:1, 2:3], 0.0),
            "pool": lambda: P.memset("pool", mk[0:1, 4:5], 0.0),
        })

    m0 = AR.mark()
    cTs = AR.alloc("cTs", [128, 8, NB], F32)
    scb = AR.alloc("scb", [128, 8, NB], BF16)
    P.dma("sp", cTs[:, :, :], cT_d.v(cT_d.h.rearrange("(k p) b -> p k b", p=128)))
    P.act(scb[:, :, :], cTs[:, :, :], AF.Silu)
    wa = [AR.alloc("wa", [128, 8, 128], BF16) for _ in range(4)]
    wacnt = [0]

    def ada_vectors(vs):
        for v in vs:
            for j in range(8):
                w = wa[wacnt[0] % 4]; wacnt[0] += 1
                c0 = (v * 8 + j) * 128
                P.dma("pool", w[:, :, :], wada_d.v(wada_d.h[:, c0:c0 + 128].rearrange("(k p) c -> p k c", p=128)))
                pb = PS.get(1)
                for k in range(8):
                    P.matmul(pb.f(0, NB), w[:, k, :], scb[:, k, :], start=(k == 0), stop=(k == 7))
                P.act(mT[:, v * 8 + j, :], pb.f(0, NB), AF.Identity, bias=badaT[:, v * 8 + j:v * 8 + j + 1], scale=1.0)

    def ada_mod(k):
        sc = mT[:, (3 * k + 1) * 8:(3 * k + 2) * 8, :]
        P.ts("dve", Amod[:, k, :, :], sc, 1.0, ALU.add)
        nb_ap = normsT.h[:, k * 8:(k + 1) * 8].unsqueeze(2).to_broadcast([128, 8, NB])
        P.tt("dve", Amod[:, k, :, :], Amod[:, k, :, :], normsT.v(nb_ap), ALU.mult)

    ada_vectors([0, 1, 2])
    ada_mod(0)
    SHv = lambda k, kc, c0, c1: mT[:, 3 * k * 8 + kc, c0:c1]
    GTv = lambda k, kc, c0, c1: mT[:, (3 * k + 2) * 8 + kc, c0:c1]

    def tile_info(ti):
        if ti < NTP:
            return ti // NT, ti % NT, False
        return None, NT, True

    def rstd_of(src, n, ss, rs, junk):
        P.act(junk, src, AF.Square, accum_out=ss)
        P.act(rs, ss, AF.Sqrt, scale=1.0 / n, bias=EPS)
        P.recip(rs, rs)

    def norm_mod(xt, k, ti, hT_out, wk):
        s, it, smp = tile_info(ti)
        rstd_of(xt[:, :], D, wk["ss"][:, 0:1], wk["rs"][:, 0:1], wk["junk"][:, :])
        P.ts("dve", wk["xn"][:, :], xt[:, :], wk["rs"][:, 0:1], ALU.mult)
        pt = PS.get(1)
        for kc in range(8):
            P.transpose(pt.bf(kc * 128, (kc + 1) * 128), wk["xn"][:, kc * 128:(kc + 1) * 128], ident_b())
        for kc in range(8):
            src = pt.bf(kc * 128, (kc + 1) * 128)
            if not smp:
                P.act(hT_out(kc), src, AF.Identity, bias=SHv(k, kc, s, s + 1), scale=Amod[:, k, kc, s:s + 1])
            else:
                tmp = wk["mtmp"]
                a_bc = Amod.v(Amod.h[:, k, kc, NPS:NB].unsqueeze(2).to_broadcast([128, 16, 8]))
                s_bc = mT.v(mT.h[:, 3 * k * 8 + kc, NPS:NB].unsqueeze(2).to_broadcast([128, 16, 8]))
                sv = V(src.bufs, src.ap.rearrange("p (b t) -> p b t", b=16))
                tv = tmp.v(tmp.h[:, :].rearrange("p (b t) -> p b t", b=16))
                o = hT_out(kc)
                ov = V(o.bufs, o.ap.rearrange("p (b t) -> p b t", b=16))
                P.tt("dve", tv, sv, a_bc, ALU.mult)
                P.tt("pool", ov, tv, s_bc, ALU.add)

    def build_gate(gate, k, ti, scale, wk):
        s, it, smp = tile_info(ti)
        for kc in range(8):
            tmp = wk["gtmp"]
            if not smp:
                src = mT.v(mT.h[:, (3 * k + 2) * 8 + kc, s:s + 1].to_broadcast([128, 128]))
                P.copy("dve", tmp[:, :], src)
            else:
                src = mT.v(mT.h[:, (3 * k + 2) * 8 + kc, NPS:NB].unsqueeze(2).to_broadcast([128, 16, 8]))
                P.copy("dve", tmp.v(tmp.h[:, :].rearrange("p (b t) -> p b t", b=16)), src)
            pb = PS.get(1)
            P.transpose(pb.f(0, 128), tmp[:, :], ident_f())
            P.act(gate[:, kc * 128:(kc + 1) * 128], pb.f(0, 128), AF.Copy, scale=scale)

    def x_src(ti):
        return xp_d[ti * 128:(ti + 1) * 128, :] if ti < NTP else xs_d[:, :]

    def rows(t, ti):
        return t[ti * 128:(ti + 1) * 128, :]

    def ffn_phase(k, wd, src_fn, dst_fn, final, tail=None, reset_to=None):
        m = AR.mark()
        GW = 4
        NG = (NF + GW - 1) // GW
        gws = [min(GW, NF - g * GW) for g in range(NG)]
        w1g = [AR.alloc("w1g", [128, 8, gws[g] * 128], BF16) for g in range(NG)]
        w3g = [AR.alloc("w3g", [128, 8, gws[g] * 128], BF16) for g in range(NG)]
        w2g = [AR.alloc("w2g", [128, gws[g], D], BF16) for g in range(NG)]
        for g in range(NG):
            c0, c1 = g * GW * 128, (g * GW + gws[g]) * 128
            for kk in range(8):
                P.dma("pool", w1g[g][:, kk, :], wd[0][kk * 128:(kk + 1) * 128, c0:c1])
                P.dma("pool", w3g[g][:, kk, :], wd[1][kk * 128:(kk + 1) * 128, c0:c1])
        for f in range(NF):
            P.dma("pool", w2g[f // GW][:, f % GW, :], wd[2][f * 128:(f + 1) * 128, :])
        w1v = lambda kk, f: w1g[f // GW][:, kk, (f % GW) * 128:(f % GW + 1) * 128]
        w3v = lambda kk, f: w3g[f // GW][:, kk, (f % GW) * 128:(f % GW + 1) * 128]
        w2v = lambda f, c0, c1: w2g[f // GW][:, f % GW, c0:c1]
        xr = [AR.alloc("xr", [128, D], F32) for _ in range(2)]
        xn_ = AR.alloc("xn", [128, D], BF16)
        wk = dict(ss=AR.alloc("ss", [128, 1], F32), rs=AR.alloc("rs", [128, 1], F32),
                  junk=xn_, xn=xn_,
                  mtmp=AR.alloc("mtmp", [128, 128], F32), gtmp=AR.alloc("gtmp", [128, 128], F32))
        nfin = None
        if final:
            nfin = AR.alloc("nfin", [128, D], F32)
            P.dma("sp", nfin[:, :], nfin_d[:, :])
        hT = AR.alloc("hT", [128, 8, TB * 128], BF16)
        gT = AR.alloc("gT", [128, NF, TB * 128], BF16)
        gate = AR.alloc("gate", [128, D], F32)
        sil = [AR.alloc("sil", [128, TB * 128], BF16) for _ in range(2)]
        tmpo = AR.alloc("tmpo", [128, 512], F32)
        blocks = []
        for s in range(NPS):
            for b0 in range(0, NT, TB):
                blocks.append([s * NT + b0 + j for j in range(TB)])
        blocks.append([NTP])
        cur_gate = None
        cnt = 0
        for tiles in blocks:
            N = 128 * len(tiles)
            gkey = tile_info(tiles[0])[0]
            if cur_gate is None or gkey != cur_gate[0]:
                build_gate(gate, k, tiles[0], 0.5, wk)
                cur_gate = (gkey,)
            for j, ti in enumerate(tiles):
                xt = xr[cnt % 2]; cnt += 1
                P.dma("sp", xt[:, :], src_fn(ti))
                norm_mod(xt, k, ti, lambda kc, j=j: hT[:, kc, j * 128:(j + 1) * 128], wk)
            for f in range(NF):
                p1 = PS.get(1); p3 = PS.get(1)
                for kk in range(8):
                    P.matmul(p1.f(0, N), w1v(kk, f), hT[:, kk, 0:N], start=(kk == 0), stop=(kk == 7))
                for kk in range(8):
                    P.matmul(p3.f(0, N), w3v(kk, f), hT[:, kk, 0:N], start=(kk == 0), stop=(kk == 7))
                sl = sil[f % 2]
                P.act(sl[:, 0:N], p1.f(0, N), AF.Silu)
                P.tt("dve", gT[:, f, 0:N], sl[:, 0:N], p3.f(0, N), ALU.mult)
            for j, ti in enumerate(tiles):
                xt = xr[cnt % 2]; cnt += 1
                P.dma("sp", xt[:, :], src_fn(ti))
                o = xt
                for half in range(2):
                    po = PS.get(1)
                    for f in range(NF):
                        P.matmul(po.f(0, 512), gT[:, f, j * 128:(j + 1) * 128], w2v(f, half * 512, (half + 1) * 512),
                                 start=(f == 0), stop=(f == NF - 1))
                    tm = tmpo
                    P.tt("dve", tm[:, :], po.f(0, 512), gate[:, half * 512:(half + 1) * 512], ALU.mult)
                    P.tt("pool", o[:, half * 512:(half + 1) * 512], tm[:, :], xt[:, half * 512:(half + 1) * 512], ALU.add)
                if final:
                    rstd_of(o[:, :], D, wk["ss"][:, 0:1], wk["rs"][:, 0:1], wk["junk"][:, :])
                    P.stt("dve", o[:, :], o[:, :], wk["rs"][:, 0:1], nfin[:, :], ALU.mult, ALU.mult)
                P.dma("sp", dst_fn(ti), o[:, :])
        if tail is not None:
            tail()
        do_barrier()
        AR.reset(m if reset_to is None else reset_to)

    class _Stop(Exception):
        pass

    def chk(level):
        if cfg.get("mix_stop", 99) == level:
            raise _Stop()

    def mixer_phase():
        m = AR.mark()
        try:
            mixer_body()
        except _Stop:
            pass
        do_barrier()
        AR.reset(m)

    def mixer_body():
        A = AR.alloc
        w_in = A("w_in", [128, 8, 2224], BF16)
        w_o = A("w_o", [128, 8, D], BF16)
        for kk in range(8):
            P.dma("pool", w_in[:, kk, :], win_d[kk * 128:(kk + 1) * 128, :])
            P.dma("pool", w_o[:, kk, :], wo_d[kk * 128:(kk + 1) * 128, :])
        gqab = A("gqab", [128, 384], F32)
        P.dma("sp", gqab[:, :], gqab_d[:, :])
        wqn = A("wqn", [128, 3, 512], BF16); wqr = A("wqr", [128, 3, 256], BF16); wqs = A("wqs", [128, 3, 256], BF16)
        for j in range(3):
            P.dma("pool", wqn[:, j, :], wqbn_d[j * 128:(j + 1) * 128, :])
            P.dma("pool", wqr[:, j, :], wqbr_d[j * 128:(j + 1) * 128, :])
            P.dma("pool", wqs[:, j, :], wqbs_d[j * 128:(j + 1) * 128, :])
        wkn = A("wkn", [64, 8, 256], BF16)
        P.dma("pool", wkn.v(wkn.h[:, :, :].rearrange("p h c -> p (h c)")), wkvbn_d[:, :])
        wkv = A("wkv", [128, 2, 8, 64], BF16)
        for cc in range(2):
            P.dma("pool", wkv.v(wkv.h[:, cc, :, :].rearrange("p h v -> p (h v)")), wkvbv_d[cc * 128:(cc + 1) * 128, :])
        wgate = A("wgate", [17, 256], F32)
        P.dma("sp", wgate[:, :], wgate_d[:, :])
        gkva = A("gkva", [128, 256], F32); ggla = A("ggla", [128, 128], F32)
        P.dma("sp", gkva[:, :], gkva_d[:, :]); P.dma("sp", ggla[:, :], ggla_d[:, :])
        ropeT = A("ropeT", [128, NT + 1, 32], F32)
        P.dma("sp", ropeT[:, :, :], ropeT_d[:, :, :])
        ropeFt = [A("ropeFt", [32, 2, 128], F32) for _ in range(2)]

        xr = [A("xr", [128, D], F32) for _ in range(2)]
        xn_ = A("xn", [128, D], BF16)
        wk = dict(ss=A("ss", [128, 1], F32), rs=A("rs", [128, 1], F32), junk=xn_, xn=xn_,
                  mtmp=A("mtmp", [128, 128], F32), gtmp=A("gtmp", [128, 128], F32))
        hT = A("hTm", [128, 8, 128], BF16)
        proj = A("proj", [128, 2224], F32)
        sm = A("sm", [128, 32], F32)
        qan = A("qan", [128, 384], BF16); qanT = A("qanT", [128, 3, 128], BF16)
        qnT = A("qnT", [64, 8, 128], BF16)
        rt1 = A("rt1", [32, 4, 128], F32); rt2 = A("rt2", [32, 4, 128], F32)
        qrT = A("qrT", [32, 8, 128], BF16)
        QlT = A("QlT", [128, 2, 8, 128], BF16)
        ckvf = A("ckvf", [128, 256], F32)
        kro = A("kro", [128, 32], F32); krt = A("krt", [128, 64], F32); krob = A("krob", [128, 32], BF16)
        PTr = [A("PT", [128, 512], BF16) for _ in range(2)]
        OT = A("OT", [128, 2, 8, 128], BF16)
        lrow = A("lrow", [1, 1024], F32)
        rl = A("rl", [128, 8], F32)
        mix = A("mix", [128, D], BF16); mixT = A("mixT", [128, 8, 128], BF16)
        gate = A("gate", [128, D], F32)
        tmpo = A("tmpo", [128, 512], F32)
        gaT1 = A("gaT1", [17, 128], F32)
        P.memset("dve", gaT1[:, :], 1.0)
        la = A("la", [128, 256], F32)
        bsb = A("bsb", [128, 256], F32); Ee = A("Ee", [128, 256], F32); Ei = A("Ei", [128, 256], F32); Dd = A("Dd", [128, 256], F32)
        e1 = Dd
        qe = A("qe", [128, 256], BF16); ke = A("ke", [128, 256], BF16); kd = A("kd", [128, 256], BF16)
        vb = A("vb", [128, 512], BF16)
        qeT = A("qeT", [128, 2, 128], BF16); keT = A("keT", [128, 2, 128], BF16)
        AmT = A("AmT", [128, 4, 128], BF16)
        osb = A("osb", [128, 512], F32); sg = A("sg", [128, 512], F32)
        ot = osb
        m_prompt = AR.mark()
        KT = A("KT", [128, 2, SEQ], BF16); KrT = A("KrT", [32, SEQ], BF16); Vq = A("Vq", [128, NT, 256], BF16)
        qe0T = A("qe0T", [128, 2, 128], BF16); qe1T = A("qe1T", [128, 2, 128], BF16)
        P.memset("pool", qe0T[:, :, :], 0.0); P.memset("pool", qe1T[:, :, :], 0.0)
        dec = A("dec", [128, 4], F32)
        S = A("S", [128, 2, 128], F32); S0b = A("S0b", [128, 2, 128], BF16); S1b = A("S1b", [128, 2, 128], BF16)
        ropecnt = [0]

        def mla_q_and_kv(ti, Vdst, KTdst, KrTdst, ckv_out, kr_out):
            s, it, smp = tile_info(ti)
            tok0 = SEQ if smp else it * 128
            rf = ropeFt[ropecnt[0] % 2]; ropecnt[0] += 1
            P.dma("sp", rf[:, :, :], ropeF_d[:, :, tok0:tok0 + 128])
            rstd_of(proj[:, 0:384], 384, sm[:, 0:1], sm[:, 1:2], wk["junk"][:, 0:384])
            P.stt("dve", qan[:, :], proj[:, 0:384], sm[:, 1:2], gqab[:, :], ALU.mult, ALU.mult)
            pq = PS.get(1)
            for j in range(3):
                P.transpose(pq.bf(j * 128, (j + 1) * 128), qan[:, j * 128:(j + 1) * 128], ident_b())
            P.copy("act", qanT.v(qanT.h[:, :, :].rearrange("p a b -> p (a b)")), pq.bf(0, 384))
            pn = PS.get(2)
            for h in range(8):
                for j in range(3):
                    P.matmul(pn.f(h * 128, (h + 1) * 128, 0, 64), wqn[:, j, h * 64:(h + 1) * 64], qanT[:, j, :],
                             start=(j == 0), stop=(j == 2))
            for b in range(2):
                P.copy("act", qnT.v(qnT.h[:, b * 4:(b + 1) * 4, :].rearrange("p a b -> p (a b)")),
                       pn.f(b * 512, (b + 1) * 512, 0, 64))
            pr = PS.get(2)
            for h in range(8):
                for j in range(3):
                    P.matmul(pr.f(h * 128, (h + 1) * 128, 0, 32), wqr[:, j, h * 32:(h + 1) * 32], qanT[:, j, :],
                             start=(j == 0), stop=(j == 2))
            pw = PS.get(2)
            for h in range(8):
                for j in range(3):
                    P.matmul(pw.f(h * 128, (h + 1) * 128, 0, 32), wqs[:, j, h * 32:(h + 1) * 32], qanT[:, j, :],
                             start=(j == 0), stop=(j == 2))
            ct_bc = rf.v(rf.h[:, 0, :].unsqueeze(1).to_broadcast([32, 4, 128]))
            sg_bc = rf.v(rf.h[:, 1, :].unsqueeze(1).to_broadcast([32, 4, 128]))
            for b in range(2):
                prv = pr.f(b * 512, (b + 1) * 512, 0, 32)
                pwv = pw.f(b * 512, (b + 1) * 512, 0, 32)
                P.tt("dve", rt1[:, :, :], V(prv.bufs, prv.ap.rearrange("p (a b) -> p a b", a=4)), ct_bc, ALU.mult)
                P.tt("dve", rt2[:, :, :], V(pwv.bufs, pwv.ap.rearrange("p (a b) -> p a b", a=4)), sg_bc, ALU.mult)
                P.tt("pool", qrT[:, b * 4:(b + 1) * 4, :], rt1[:, :, :], rt2[:, :, :], ALU.add)
            pl = PS.get(4)
            for cc in range(2):
                for h in range(8):
                    c0 = (cc * 8 + h) * 128
                    P.matmul(pl.f(c0, c0 + 128), wkn[:, h, cc * 128:(cc + 1) * 128], qnT[:, h, :])
            for b in range(4):
                cc, hb = b // 2, (b % 2) * 4
                P.copy("act" if b % 2 == 0 else "dve",
                       QlT.v(QlT.h[:, cc, hb:hb + 4, :].rearrange("p a b -> p (a b)")), pl.f(b * 512, (b + 1) * 512))
            rstd_of(proj[:, 384:640], 256, sm[:, 2:3], sm[:, 3:4], wk["junk"][:, 0:256])
            P.stt("dve", ckvf[:, :], proj[:, 384:640], sm[:, 3:4], gkva[:, :], ALU.mult, ALU.mult)
            P.dma("sp", ckv_out, ckvf[:, :])
            P.copy("act", Vdst, ckvf[:, :])
            cosv = ropeT[:, it, 0:16]; sinv = ropeT[:, it, 16:32]
            x1 = proj[:, 640:656]; x2 = proj[:, 656:672]
            P.tt("pool", krt[:, 0:16], x1, cosv, ALU.mult)
            P.tt("pool", krt[:, 16:32], x2, sinv, ALU.mult)
            P.tt("pool", krt[:, 32:48], x1, sinv, ALU.mult)
            P.tt("pool", krt[:, 48:64], x2, cosv, ALU.mult)
            P.tt("pool", kro[:, 0:16], krt[:, 0:16], krt[:, 16:32], ALU.subtract)
            P.tt("pool", kro[:, 16:32], krt[:, 32:48], krt[:, 48:64], ALU.add)
            P.dma("sp", kr_out, kro[:, :])
            P.copy("pool", krob[:, :], kro[:, :])
            pk = PS.get(1)
            P.transpose(pk.bf(0, 128), V(Vdst.bufs, Vdst.ap[:, 0:128]), ident_b())
            P.transpose(pk.bf(128, 256), V(Vdst.bufs, Vdst.ap[:, 128:256]), ident_b())
            P.transpose(pk.bf(256, 384, 0, 32), krob[:, :], ident_b())
            P.copy("act", KTdst(0), pk.bf(0, 128))
            P.copy("dve", KTdst(1), pk.bf(128, 256))
            P.copy("act", KrTdst, pk.bf(256, 384, 0, 32))

        def attend_blocks(blocks, OTd, ld):
            nblk = len(blocks)
            for g in range(2):
                pO = PS.get(2, hold=True)
                pL = PS.get(1, hold=True)
                for j, (k0, k1, kr, vfn, mask) in enumerate(blocks):
                    pS = PS.get(1)
                    q0 = QlT.v(QlT.h[:, 0, g * 4:(g + 1) * 4, :].rearrange("p a b -> p (a b)"))
                    q1 = QlT.v(QlT.h[:, 1, g * 4:(g + 1) * 4, :].rearrange("p a b -> p (a b)"))
                    qr = qrT.v(qrT.h[:, g * 4:(g + 1) * 4, :].rearrange("p a b -> p (a b)"))
                    P.matmul(pS.f(0, 512), k0, q0, start=True, stop=False)
                    P.matmul(pS.f(0, 512), k1, q1, start=False, stop=False)
                    P.matmul(pS.f(0, 512), kr, qr, start=False, stop=True)
                    pt = PTr[j % 2]
                    P.act(pt[:, :], pS.f(0, 512), AF.Exp, scale=MLA_SCALE)
                    if mask is not None:
                        mb = V(mask.bufs, mask.ap.unsqueeze(1).to_broadcast([128, 4, 128]))
                        ptv = pt.v(pt.h[:, :].rearrange("p (a b) -> p a b", a=4))
                        P.tt("pool", ptv, ptv, mb, ALU.mult)
                    for cc in range(2):
                        P.matmul(pO.f(cc * 512, (cc + 1) * 512), vfn(cc), pt[:, :], start=(j == 0), stop=(j == nblk - 1))
                    P.matmul(pL.f(0, 512, 0, 1), cstb[:, C_ONE:C_ONE + 1], pt[:, :], start=(j == 0), stop=(j == nblk - 1))
                for cc in range(2):
                    P.copy("act" if cc == 0 else "dve",
                           OTd.v(OTd.h[:, cc, g * 4:(g + 1) * 4, :].rearrange("p a b -> p (a b)")),
                           pO.f(cc * 512, (cc + 1) * 512))
                P.copy("act", ld[0:1, g * 512:(g + 1) * 512], pL.f(0, 512, 0, 1))
                PS.release(pO); PS.release(pL)

        def o_mla(OTs, ls):
            if ls is not None:
                plt = PS.get(1)
                for h in range(8):
                    P.matmul(plt.f(h, h + 1), ls[0:1, h * 128:(h + 1) * 128], cst[0:1, C_ONE:C_ONE + 1])
                P.recip(rl[:, :], plt.f(0, 8))
            po = PS.get(1)
            for h in range(8):
                for cc in range(2):
                    P.matmul(po.f(h * 64, (h + 1) * 64), OTs[:, cc, h, :], wkv[:, cc, h, :], start=(cc == 0), stop=(cc == 1))
            pov = po.f(0, 512)
            if ls is None:
                P.copy("act", mix[:, 0:512], pov)
                return
            P.tt("dve", mix.v(mix.h[:, 0:512].rearrange("p (a b) -> p a b", a=8)),
                 V(pov.bufs, pov.ap.rearrange("p (a b) -> p a b", a=8)),
                 rl.v(rl.h[:, :].unsqueeze(2).to_broadcast([128, 8, 64])), ALU.mult)

        def gla_common(smp):
            tri = cst[:, C_SBLK:C_SBLK + 128] if smp else cst[:, C_TRIP:C_TRIP + 128]
            ones = cst[:, C_OBLK8:C_OBLK8 + 128] if smp else cst[:, C_ONESP:C_ONESP + 128]
            pg = PS.get(1)
            P.transpose(pg.f(0, 128, 0, 16), proj[:, 1696:1712], ident_f())
            P.copy("act", gaT1[0:16, :], pg.f(0, 128, 0, 16))
            px = PS.get(1)
            P.matmul(px.f(0, 256), gaT1[0:17, :], wgate[0:17, :])
            P.act(e1[:, :], px.f(0, 256), AF.Exp, scale=-1.0)
            P.act(la[:, :], e1[:, :], AF.Ln, bias=1.0)
            P.ts("dve", la[:, :], la[:, :], -1.0 / 16.0, ALU.mult)
            chk(401)
            pb = PS.get(1)
            P.matmul(pb.f(0, 256), tri, la[:, :])
            P.matmul(pb.f(256, 512), ones, la[:, :])
            P.copy("act", bsb[:, :], pb.f(0, 256))
            P.act(Ee[:, :], bsb[:, :], AF.Exp)
            P.act(Ei[:, :], bsb[:, :], AF.Exp, scale=-1.0)
            P.tt("dve", Dd[:, :], pb.f(256, 512), bsb[:, :], ALU.subtract)
            P.act(Dd[:, :], Dd[:, :], AF.Exp)
            chk(402)
            P.stt("dve", qe[:, :], proj[:, 672:928], 0.125, Ee[:, :], ALU.mult, ALU.mult)
            P.tt("dve", ke[:, :], proj[:, 928:1184], Ei[:, :], ALU.mult)
            P.tt("dve", kd[:, :], proj[:, 928:1184], Dd[:, :], ALU.mult)
            P.copy("act", vb[:, :], proj[:, 1184:1696])
            chk(403)
            pt2 = PS.get(1)
            for p in range(2):
                P.transpose(pt2.bf(p * 128, (p + 1) * 128), qe[:, p * 128:(p + 1) * 128], ident_b())
                P.transpose(pt2.bf(256 + p * 128, 256 + (p + 1) * 128), ke[:, p * 128:(p + 1) * 128], ident_b())
            chk(4031)
            P.copy("act", qeT.v(qeT.h[:, :, :].rearrange("p a b -> p (a b)")), pt2.bf(0, 256))
            chk(4032)
            P.copy("act", keT.v(keT.h[:, :, :].rearrange("p a b -> p (a b)")), pt2.bf(256, 512))
            chk(404)
            pA = [PS.get(1), PS.get(1)]
            for hh in range(2):
                for p in range(2):
                    P.matmul(pA[hh].f(p * 128, (p + 1) * 128), keT[64 * hh:64 * hh + 64, p, :], qeT[64 * hh:64 * hh + 64, p, :])
            for hh in range(2):
                for p in range(2):
                    P.copy("act", AmT[:, 2 * p + hh, :], pA[hh].f(p * 128, (p + 1) * 128))
            trib = cstb[:, C_SBLK:C_SBLK + 128] if smp else cstb[:, C_TRIP:C_TRIP + 128]
            P.tt("pool", AmT[:, :, :], AmT[:, :, :], V(trib.bufs, trib.ap.unsqueeze(1).to_broadcast([128, 4, 128])), ALU.mult)

        def gla_finish(po2):
            for h in range(4):
                P.copy("act", osb[:, h * 128:(h + 1) * 128], po2[h])
            for h in range(4):
                P.act(wk["junk"][:, 0:128], osb[:, h * 128:(h + 1) * 128], AF.Square, accum_out=sm[:, 8 + h:9 + h])
            P.act(sm[:, 12:16], sm[:, 8:12], AF.Sqrt, scale=1.0 / 128, bias=EPS)
            P.recip(sm[:, 12:16], sm[:, 12:16])
            P.act(sg[:, :], proj[:, 1712:2224], AF.Silu)
            sgv = sg.v(sg.h[:, :].rearrange("p (a b) -> p a b", a=4))
            P.tt("pool", sgv, sgv, ggla.v(ggla.h[:, :].unsqueeze(1).to_broadcast([128, 4, 128])), ALU.mult)
            osv = osb.v(osb.h[:, :].rearrange("p (a b) -> p a b", a=4))
            P.tt("dve", osv, osv, sm.v(sm.h[:, 12:16].unsqueeze(2).to_broadcast([128, 4, 128])), ALU.mult)
            P.tt("pool", mix[:, 512:1024], osb[:, :], sg[:, :], ALU.mult)

        def gla_prompt():
            gla_common(False)
            chk(41)
            P.copy("pool", qe0T[:, :, 0:64], qeT[:, :, 0:64])
            P.copy("pool", qe1T[:, :, 64:128], qeT[:, :, 64:128])
            pd = PS.get(1)
            for p in range(2):
                P.matmul(pd.f(p * 2, p * 2 + 2), la[:, p * 128:(p + 1) * 128], cst[:, C_CHK:C_CHK + 2])
            P.act(dec[:, :], pd.f(0, 4), AF.Exp)
            P.copy("act", S0b[:, :, :], S[:, :, :])
            chk(42)

            def update(c):
                for p in range(2):
                    pu = PS.get(1)
                    P.matmul(pu.f(0, 256), kd[64 * c:64 * c + 64, p * 128:(p + 1) * 128], vb[64 * c:64 * c + 64, p * 256:(p + 1) * 256])
                    P.stt("dve", S[0:64, p, :], S[0:64, p, :], dec[0:64, 2 * p + c:2 * p + c + 1], pu.f(0, 128, 0, 64), ALU.mult, ALU.add)
                    P.stt("dve", S[64:128, p, :], S[64:128, p, :], dec[64:128, 2 * p + c:2 * p + c + 1], pu.f(128, 256, 64, 128), ALU.mult, ALU.add)
            update(0)
            chk(43)
            P.copy("act", S1b[:, :, :], S[:, :, :])
            pob = [PS.get(1, hold=True), PS.get(1, hold=True)]
            po2 = [None] * 4
            for hh in range(2):
                r = slice(64 * hh, 64 * hh + 64)
                for p in range(2):
                    h = 2 * p + hh
                    o_ = pob[hh].f(p * 128, (p + 1) * 128)
                    po2[h] = o_
                    P.matmul(o_, AmT[:, h, :], vb[:, h * 128:(h + 1) * 128], start=True, stop=False)
                    P.matmul(o_, qe0T[r, p, :], S0b[r, p, :], start=False, stop=False)
                    P.matmul(o_, qe1T[r, p, :], S1b[r, p, :], start=False, stop=True)
            chk(44)
            update(1)
            chk(45)
            gla_finish(po2)
            PS.release(pob[0]); PS.release(pob[1])

        def out_proj(ti, xt):
            if dbg:
                P.dma("pool", rows(MIXD, ti), mix[:, :])
            pm = PS.get(1)
            for kc in range(8):
                P.transpose(pm.bf(kc * 128, (kc + 1) * 128), mix[:, kc * 128:(kc + 1) * 128], ident_b())
            P.copy("act", mixT.v(mixT.h[:, :, :].rearrange("p a b -> p (a b)")), pm.bf(0, 1024))
            for half in range(2):
                pw_ = PS.get(1)
                for kc in range(8):
                    P.matmul(pw_.f(0, 512), mixT[:, kc, :], w_o[:, kc, half * 512:(half + 1) * 512], start=(kc == 0), stop=(kc == 7))
                P.tt("dve", tmpo[:, :], pw_.f(0, 512), gate[:, half * 512:(half + 1) * 512], ALU.mult)
                P.tt("pool", xt[:, half * 512:(half + 1) * 512], tmpo[:, :], xt[:, half * 512:(half + 1) * 512], ALU.add)
            P.dma("sp", rows(X2, ti), xt[:, :])

        def in_proj(ti):
            xt = xr[ti % 2]
            P.dma("sp", xt[:, :], rows(X1, ti))
            norm_mod(xt, 1, ti, lambda kc: hT[:, kc, :], wk)
            pp = PS.get(5)
            for cg in range(5):
                c0, c1 = cg * 512, min(2224, cg * 512 + 512)
                for kk in range(8):
                    P.matmul(pp.f(c0, c1), hT[:, kk, :], w_in[:, kk, c0:c1], start=(kk == 0), stop=(kk == 7))
            for cg in range(5):
                c0, c1 = cg * 512, min(2224, cg * 512 + 512)
                P.copy("act" if cg % 2 == 0 else "dve", proj[:, c0:c1], pp.f(c0, c1))
            return xt

        for s in range(NPS):
            P.memset("dve", S[:, :, :], 0.0)
            build_gate(gate, 1, s * NT, 1.0, wk)
            for it in range(NT):
                ti = s * NT + it
                xt = in_proj(ti)
                chk(1)
                mla_q_and_kv(ti, Vq[:, it, :], lambda cc, it=it: KT[:, cc, it * 128:(it + 1) * 128],
                             KrT[:, it * 128:(it + 1) * 128], rows(ckvp_d, ti), rows(krp_d, ti))
                chk(2)
                blocks = []
                for j in range(it + 1):
                    blocks.append((KT[:, 0, j * 128:(j + 1) * 128], KT[:, 1, j * 128:(j + 1) * 128], KrT[:, j * 128:(j + 1) * 128],
                                   (lambda cc, j=j: Vq[:, j, cc * 128:(cc + 1) * 128]),
                                   cstb[:, C_CAUSAL:C_CAUSAL + 128] if j == it else None))
                gla_prompt()
                chk(5)
                attend_blocks(blocks, OT, lrow)
                chk(3)
                o_mla(OT, lrow)
                chk(4)
                out_proj(ti, xt)
            for p in range(2):
                P.dma("sp", glap_d.v(glap_d.h[s, 2 * p:2 * p + 2, :, :].rearrange("h k v -> (h k) v")), S[:, p, :])

        chk(6)
        do_barrier()
        AR.reset(m_prompt)
        ti = NTP
        build_gate(gate, 1, ti, 1.0, wk)
        Vn = A("Vn", [128, 256], BF16); KTn = A("KTn", [128, 2, 128], BF16); KrTn = A("KrTn", [32, 128], BF16)
        ps4 = [A("ps4", [128, 64], F32) for _ in range(2)]
        olat = A("olat", [64, 256], BF16); rlb = A("rlb", [64, 1], F32)
        ptT = A("ptT", [128, NSS], I32)
        P.dma("sp", ptT[0:NPG, :], ptT_d[:, :])
        NTG = 8
        KVg = [A("KVg", [128, NTG, 256], BF16) for _ in range(2)]
        KRg = [A("KRg", [128, NTG, 32], BF16) for _ in range(2)]
        KTc = [A("KTc", [128, 8, 128], BF16) for _ in range(2)]
        KRTc = [A("KRTc", [32, 4, 128], BF16) for _ in range(2)]
        PTs = [A("PTs", [128, 256], BF16) for _ in range(2)]
        S0s = A("S0s", [128, 16, 128], F32); S0sb = A("S0sb", [128, 16, 128], BF16)
        decS = A("decS", [128, 2, 16], F32)
        Vblk = A("Vblk", [128, 8, 256], BF16)
        oTs = A("oTs", [128, 4, 128], F32)
        xt = in_proj(ti)
        mla_q_and_kv(ti, Vn[:, 0:256], lambda cc: KTn[:, cc, :], KrTn[:, :], ckvs_d[:, :], krs_d[:, :])
        chk(7)
        gcnt = 0
        ccnt = 0
        NG = PAGE // NTG
        for b in range(NSS):
            pOb = PS.get(1, hold=True)
            pLb = PS.get(1, hold=True)
            q0 = QlT[:, 0, :, 8 * b:8 * b + 8]; q1 = QlT[:, 1, :, 8 * b:8 * b + 8]; qr = qrT[:, :, 8 * b:8 * b + 8]
            pS = PS.get(1)
            pts = PTs[ccnt % 2]; ccnt += 1
            P.matmul(pS.f(0, 64), KTn[:, 0, :], q0, start=True, stop=False)
            P.matmul(pS.f(0, 64), KTn[:, 1, :], q1, start=False, stop=False)
            P.matmul(pS.f(0, 64), KrTn[:, :], qr, start=False, stop=True)
            P.act(pts[:, 0:64], pS.f(0, 64), AF.Exp, scale=MLA_SCALE)
            pv_ = pts.v(pts.h[:, 0:64].rearrange("p (a b) -> p a b", a=8))
            P.tt("pool", pv_, pv_, cstb.v(cstb.h[:, C_SBLK + 8 * b:C_SBLK + 8 * b + 8].unsqueeze(1).to_broadcast([128, 8, 8])), ALU.mult)
            P.matmul(pOb.f(0, 256, 0, 64), pts[:, 0:64], Vn[:, 0:256], start=True, stop=False)
            P.matmul(pLb.f(0, 1, 0, 64), pts[:, 0:64], cstb[:, C_ONE:C_ONE + 1], start=True, stop=False)
            for gi in range(NG):
                kv = KVg[gcnt % 2]; krg = KRg[gcnt % 2]; gcnt += 1
                P.idma(kv.v(kv.h[0:KP, :, :].rearrange("p t c -> p (t c)")), ckvc_d[:, :], ptT[0:KP, b:b + 1],
                       element_offset=gi * NTG * 256)
                P.idma(krg.v(krg.h[0:KP, :, :].rearrange("p t c -> p (t c)")), krc_d[:, :], ptT[0:KP, b:b + 1],
                       element_offset=gi * NTG * 32)
                for sb in range(NTG // 4):
                    ktc = KTc[ccnt % 2]; krtc = KRTc[ccnt % 2]; pts = PTs[ccnt % 2]; ccnt += 1
                    pk = PS.get(1); pkr = PS.get(1)
                    for q in range(4):
                        t = sb * 4 + q
                        for cc in range(2):
                            c0 = (q * 2 + cc) * 128
                            P.transpose(pk.bf(c0, c0 + KP), kv[0:KP, t, cc * 128:(cc + 1) * 128], ident_b(KP))
                        P.transpose(pkr.bf(q * 128, q * 128 + KP, 0, 32), krg[0:KP, t, :], ident_b(KP))
                    if KP == 128:
                        P.copy("act", ktc.v(ktc.h[:, :, :].rearrange("p a b -> p (a b)")), pk.bf(0, 1024))
                        P.copy("act", krtc.v(krtc.h[:, :, :].rearrange("p a b -> p (a b)")), pkr.bf(0, 512, 0, 32))
                    else:
                        pkv = pk.bf(0, 1024); pkrv = pkr.bf(0, 512, 0, 32)
                        P.copy("act", ktc[:, :, 0:KP], V(pkv.bufs, pkv.ap.rearrange("p (a b) -> p a b", a=8)[:, :, 0:KP]))
                        P.copy("act", krtc[:, :, 0:KP], V(pkrv.bufs, pkrv.ap.rearrange("p (a b) -> p a b", a=4)[:, :, 0:KP]))
                    pS = PS.get(1)
                    for q in range(4):
                        o = pS.f(q * 64, (q + 1) * 64, 0, KP)
                        P.matmul(o, ktc[:, q * 2, 0:KP], q0, start=True, stop=False)
                        P.matmul(o, ktc[:, q * 2 + 1, 0:KP], q1, start=False, stop=False)
                        P.matmul(o, krtc[:, q, 0:KP], qr, start=False, stop=True)
                    P.act(pts[0:KP, :], pS.f(0, 256, 0, KP), AF.Exp, scale=MLA_SCALE)
                    for q in range(4):
                        t = sb * 4 + q
                        last = (gi == NG - 1) and (sb == NTG // 4 - 1) and (q == 3)
                        P.matmul(pOb.f(0, 256, 0, 64), pts[0:KP, q * 64:(q + 1) * 64], kv[0:KP, t, 0:256], start=False, stop=last)
                    p4 = ps4[ccnt % 2]
                    P.op("dve", lambda e, p4=p4, pts=pts: e.tensor_reduce(
                        p4.h[0:KP, :], pts.h[0:KP, :].rearrange("p (c q) -> p q c", c=4), AX.X, ALU.add),
                        [pts[0:KP, :]], [p4[0:KP, :]])
                    lastsb = (gi == NG - 1) and (sb == NTG // 4 - 1)
                    P.matmul(pLb.f(0, 1, 0, 64), p4[0:KP, :], cst[0:KP, C_ONE:C_ONE + 1], start=False, stop=lastsb)
            P.recip(rlb[:, :], pLb.f(0, 1, 0, 64))
            P.ts("dve", olat[:, :], pOb.f(0, 256, 0, 64), rlb[:, 0:1], ALU.mult)
            PS.release(pOb); PS.release(pLb)
            pto = PS.get(1)
            for cc in range(2):
                P.transpose(pto.bf(cc * 64, cc * 64 + 64), olat[:, cc * 128:(cc + 1) * 128], ident_b(64))
            for cc in range(2):
                pv = pto.bf(cc * 64, cc * 64 + 64)
                P.copy("act", OT[:, cc, :, 8 * b:8 * b + 8], V(pv.bufs, pv.ap.rearrange("p (a b) -> p a b", a=8)))
        chk(8)
        o_mla(OT, None)
        gla_common(True)
        pd = PS.get(1)
        for p in range(2):
            P.matmul(pd.f(p * 16, p * 16 + 16), la[:, p * 128:(p + 1) * 128], cst[:, C_SEG:C_SEG + 16])
        P.act(decS.v(decS.h[:, :, :].rearrange("p a b -> p (a b)")), pd.f(0, 32), AF.Exp)
        poTb = [PS.get(1, hold=True), PS.get(1, hold=True)]
        poT = lambda h, c0, c1: poTb[h % 2].f((h // 2) * 128 + c0, (h // 2) * 128 + c1)
        for p in range(2):
            for hh in range(2):
                P.dma("sp", S0s[64 * hh:64 * hh + 64, :, :],
                      state_d.v(state_d.h[:, 2 * p + hh, :, :].rearrange("b k v -> k b v")))
            P.copy("act", S0sb.v(S0sb.h[:, :, :].rearrange("p a b -> p (a b)")),
                   S0s.v(S0s.h[:, :, :].rearrange("p a b -> p (a b)")))
            for hh in range(2):
                h = 2 * p + hh
                r = slice(64 * hh, 64 * hh + 64)
                for b in range(NSS):
                    P.matmul(poT(h, 8 * b, 8 * b + 8), S0sb[r, b, :], qeT[r, p, 8 * b:8 * b + 8], start=(b == 0), stop=False)
                P.matmul(poT(h, 0, 128), vb[:, h * 128:(h + 1) * 128], AmT[:, h, :], start=False, stop=True)
            for half in range(2):
                P.tt("dve", Vblk[:, :, :], vb.v(vb.h[:, p * 256:(p + 1) * 256].unsqueeze(1).to_broadcast([128, 8, 256])),
                     cst.v(cst.h[:, C_SEG + 8 * half:C_SEG + 8 * half + 8].unsqueeze(2).to_broadcast([128, 8, 256])), ALU.mult)
                for bb in range(4):
                    pu = PS.get(1)
                    P.matmul(pu.f(0, 512), kd[:, p * 128:(p + 1) * 128], Vblk.v(Vblk.h[:, 2 * bb:2 * bb + 2, :].rearrange("p a b -> p (a b)")))
                    for bl in range(2):
                        b = half * 8 + 2 * bb + bl
                        off = bl * 256
                        P.stt("dve", S0s[0:64, b, :], S0s[0:64, b, :], decS[0:64, p, b:b + 1], pu.f(off, off + 128, 0, 64), ALU.mult, ALU.add)
                        P.stt("dve", S0s[64:128, b, :], S0s[64:128, b, :], decS[64:128, p, b:b + 1], pu.f(off + 128, off + 256, 64, 128), ALU.mult, ALU.add)
            for hh in range(2):
                P.dma("sp", glas_d.v(glas_d.h[:, 2 * p + hh, :, :].rearrange("b k v -> k b v")), S0s[64 * hh:64 * hh + 64, :, :])
        for h in range(4):
            P.copy("act", oTs[:, h, :], poT(h, 0, 128))
        PS.release(poTb[0]); PS.release(poTb[1])
        po2b = PS.get(1)
        for h in range(4):
            P.transpose(po2b.f(h * 128, (h + 1) * 128), oTs[:, h, :], ident_f())
        gla_finish([po2b.f(h * 128, (h + 1) * 128) for h in range(4)])
        out_proj(ti, xt)


    if "ffn1" in phases:
        def ada_rest():
            ada_vectors([3, 4, 5, 6, 7, 8])
            ada_mod(1)
            ada_mod(2)
        ffn_phase(0, ffn_d[0], x_src, lambda ti: rows(X1, ti), False, tail=ada_rest, reset_to=m0)
    if "mixer" in phases:
        mixer_phase()
    if "ffn2" in phases:
        ffn_phase(2, ffn_d[1], lambda ti: rows(X2, ti), lambda ti: (rows(yp_d, ti) if ti < NTP else ys_d[:, :]), True)
    P.emit()
    return nc


def prep_inputs(inp, cfg, n_cores):
    NPS, SEQ, NSS, NPG, NPHYS, DFF = cfg["NPS"], cfg["SEQ"], cfg["NSS"], cfg["NPG"], cfg["NPHYS"], cfg["DFF"]
    f = lambda a: np.ascontiguousarray(np.asarray(a), dtype=np.float32)
    col = lambda v: np.ascontiguousarray(np.asarray(v, np.float32).reshape(-1, 128).T)
    shared = {}
    shared["cache_ckv"] = f(inp["cache_ckv"][0]).reshape(NPHYS, PAGE * 256)
    shared["cache_krope"] = f(inp["cache_krope"][0]).reshape(NPHYS, PAGE * 32)
    shared["w_ada"] = f(inp["w_ada"][0])
    shared["b_adaT"] = col(inp["b_ada"][0])
    shared["normsT"] = np.ascontiguousarray(np.concatenate(
        [col(inp["norm_ffn1"][0]), col(inp["norm_mix"][0]), col(inp["norm_ffn2"][0])], axis=1))
    shared["nfin_bc"] = np.ascontiguousarray(np.broadcast_to(f(inp["norm_final"]).reshape(1, -1), (128, 1024)))
    for i, nm in ((1, "ffn1"), (2, "ffn2")):
        shared["f%d_w1" % i] = f(inp[nm + "_w1"][0]); shared["f%d_w3" % i] = f(inp[nm + "_w3"][0]); shared["f%d_w2" % i] = f(inp[nm + "_w2"][0])
    shared["w_in"] = f(inp["w_in"][0])
    shared["g_qa_bc"] = np.ascontiguousarray(np.broadcast_to(f(inp["g_qa"][0]).reshape(1, -1), (128, 384)))
    wqb = f(inp["w_qb"][0])
    shared["wqb_n"] = np.ascontiguousarray(wqb[:, :, 0:64].reshape(384, 512))
    shared["wqb_r"] = np.ascontiguousarray(wqb[:, :, 64:96].reshape(384, 256))
    shared["wqb_s"] = np.ascontiguousarray(np.concatenate([wqb[:, :, 80:96], wqb[:, :, 64:80]], axis=2).reshape(384, 256))
    shared["g_kva_bc"] = np.ascontiguousarray(np.broadcast_to(f(inp["g_kva"][0]).reshape(1, -1), (128, 256)))
    wkvb = f(inp["w_kvb"][0])
    shared["wkvb_nT"] = np.ascontiguousarray(wkvb[:, :, 0:64].transpose(2, 1, 0).reshape(64, 8 * 256))
    shared["wkvb_v"] = np.ascontiguousarray(wkvb[:, :, 64:128].reshape(256, 8 * 64))
    shared["wgate"] = np.ascontiguousarray(np.concatenate([f(inp["w_gate_b"][0]), f(inp["b_gate"][0]).reshape(1, -1)], axis=0))
    shared["g_gla_bc"] = np.ascontiguousarray(np.broadcast_to(f(inp["g_gla_o"][0]).reshape(1, -1), (128, 128)))
    shared["w_o"] = f(inp["w_o"][0])
    shared["consts"] = make_consts()
    tok, feat = rope_tables(SEQ, NPG * PAGE)
    shared["rope_tok"] = tok
    shared["rope_feat"] = feat
    xp = f(inp["x_prompt"]); xs = f(inp["x_sample"])
    cp = f(inp["c_prompt"]); cs = f(inp["c_sample"])
    st = f(inp["state_gla"][0])
    pt = np.asarray(inp["page_table"]).astype(np.int32)
    maps = []
    for c in range(n_cores):
        d = dict(shared)
        d["xp"] = np.ascontiguousarray(xp[c * NPS:(c + 1) * NPS].reshape(NPS * SEQ, 1024))
        d["xs"] = np.ascontiguousarray(xs[c * NSS:(c + 1) * NSS].reshape(NSS * 8, 1024))
        d["state"] = np.ascontiguousarray(st[c * NSS:(c + 1) * NSS])
        d["ptT"] = np.ascontiguousarray(pt[c * NSS:(c + 1) * NSS].T)
        d["cT"] = np.ascontiguousarray(np.concatenate([cp[c * NPS:(c + 1) * NPS], cs[c * NSS:(c + 1) * NSS]], axis=0).T)
        maps.append(d)
    return maps


def assemble(results, cfg, n_cores):
    NPS, SEQ, NSS = cfg["NPS"], cfg["SEQ"], cfg["NSS"]
    cat = lambda k: np.concatenate([np.asarray(r[k]) for r in results], axis=0)
    y_p = cat("y_p").reshape(n_cores * NPS, SEQ, 1024)
    y_s = cat("y_s").reshape(n_cores * NSS, 8, 1024)
    ckv_p = cat("ckv_p").reshape(1, n_cores * NPS, SEQ, 256)
    kr_p = cat("kr_p").reshape(1, n_cores * NPS, SEQ, 32)
    gla_p = cat("gla_p").reshape(1, n_cores * NPS, 4, 64, 128)
    ckv_s = cat("ckv_s").reshape(1, n_cores * NSS, 8, 256)
    kr_s = cat("kr_s").reshape(1, n_cores * NSS, 8, 32)
    gla_s = cat("gla_s").reshape(1, n_cores * NSS, 4, 64, 128)
    return tuple(np.ascontiguousarray(a, dtype=np.float32) for a in (y_p, y_s, ckv_p, kr_p, gla_p, ckv_s, kr_s, gla_s))


FULL_CFG = dict(NPS=2, SEQ=2048, NSS=16, NPG=128, NPHYS=20480, DFF=2816)


def kernel(**inputs):
    cfg = dict(FULL_CFG)
    n_cores = 8
    nc = build_program(cfg)
    maps = prep_inputs(inputs, cfg, n_cores)
    res = run_bass_kernel_spmd(nc, maps, core_ids=list(range(n_cores)))
    return assemble(res.results, cfg, n_cores)
```

```python
import contextlib
import numpy as np
import concourse.bass as bass
import concourse.mybir as mybir
from concourse.bass_utils import run_bass_kernel_spmd

F32 = mybir.dt.float32
BF16 = mybir.dt.bfloat16
I32 = mybir.dt.int32
ALU = mybir.AluOpType
AF = mybir.ActivationFunctionType
AX = mybir.AxisListType

N_DMA_SEMS = 24
LISTING = None
ENGS = ("pe", "act", "dve", "pool", "sp")
EPS = 1e-6
MLA_SCALE = 96.0 ** -0.5
ROPE_THETA = 10000.0
PAGE = 128


class Buf:
    __slots__ = ("name", "last_w", "readers")

    def __init__(self, name):
        self.name = name
        self.last_w = None
        self.readers = []


class V:
    __slots__ = ("bufs", "ap")

    def __init__(self, bufs, ap):
        self.bufs = bufs
        self.ap = ap


class T:
    def __init__(self, name, handle, bufs=None):
        self.name = name
        self.h = handle
        self.bufs = bufs or (Buf(name),)

    def __getitem__(self, idx):
        return V(self.bufs, self.h[idx])

    def v(self, ap):
        return V(self.bufs, ap)


class Instr:
    __slots__ = ("eng", "fn", "deps", "odeps", "signal", "is_dma", "dma_sem", "dma_val", "ticket",
                 "cost", "seq", "pos", "nbytes", "succ", "nun", "avail", "done")

    def __init__(self, eng, fn):
        self.eng = eng
        self.fn = fn
        self.deps = []
        self.odeps = []
        self.signal = False
        self.is_dma = False
        self.dma_sem = None
        self.dma_val = 0
        self.ticket = 0
        self.cost = 0.2
        self.seq = 0
        self.pos = 0
        self.nbytes = 0


class Prog:
    def __init__(self, nc):
        self.nc = nc
        self.streams = {e: [] for e in ENGS}
        self.stack = contextlib.ExitStack()
        self.dma_ring = {e: 0 for e in ENGS}
        self.dma_count = {}
        self.dma_last = {}
        self.segs = [{e: [] for e in ENGS}]
        self.tails = []
        self.nseq = 0

    def dram(self, name, shape, dtype, kind):
        h = self.nc.dram_tensor(name, list(shape), dtype, kind=kind)
        return T(name, h.ap())

    def _record(self, eng, fn, reads, writes, is_dma=False, cost=0.2, nbytes=0):
        ins = Instr(eng, fn)
        ins.is_dma = is_dma
        ins.cost = cost
        ins.nbytes = nbytes
        ins.seq = self.nseq
        self.nseq += 1
        seen = {}
        def add(d, kind):
            if d is ins:
                return
            k = id(d)
            if k in seen:
                if kind != "war":
                    seen[k][1] = kind
                return
            ent = [d, kind]
            seen[k] = ent
            ins.odeps.append(ent)
        for v in reads:
            for b in v.bufs:
                if b.last_w is not None:
                    add(b.last_w, "raw")
        for v in writes:
            for b in v.bufs:
                if b.last_w is not None:
                    add(b.last_w, "waw")
                for r in b.readers:
                    add(r, "war")
        if is_dma:
            slot = self.dma_ring[eng]
            self.dma_ring[eng] = (slot + 1) % N_DMA_SEMS
            key = (eng, slot)
            prev = self.dma_last.get(key)
            if prev is not None:
                add(prev, "raw")
            self.dma_count[key] = self.dma_count.get(key, 0) + 1
            ins.dma_sem = key
            ins.dma_val = 16 * self.dma_count[key]
            self.dma_last[key] = ins
        for v in reads:
            for b in v.bufs:
                b.readers.append(ins)
        for v in writes:
            for b in v.bufs:
                b.last_w = ins
                b.readers = []
        self.segs[-1][eng].append(ins)
        return ins

    @staticmethod
    def _free(ap):
        n = 1
        for d in ap.shape[1:]:
            n *= d
        return n

    def _ecost(self, eng, out):
        n = self._free(out.ap)
        if eng == "act":
            return 0.2 + n / 1200.0
        if eng == "dve":
            return 0.1 + n / 960.0
        return 0.2 + n / 500.0

    def op(self, eng, fn, reads=(), writes=(), cost=None):
        writes = list(writes)
        if cost is None:
            cost = self._ecost(eng, writes[0]) if writes else 0.2
        return self._record(eng, fn, list(reads), writes, cost=cost)

    def dma(self, eng, out, in_, **kw):
        nb = out.ap.shape[0] * self._free(out.ap) * _SZ.get(in_.ap.dtype, 4)
        return self._record(eng, lambda e: e.dma_start(out=out.ap, in_=in_.ap, **kw), [in_], [out], is_dma=True,
                            cost=(1.0 if eng == "pool" else 0.1), nbytes=nb)

    def idma(self, out, in_, idx, **kw):
        nb = out.ap.shape[0] * self._free(out.ap) * 4
        return self._record("pool", lambda e: e.indirect_dma_start(
            out=out.ap, out_offset=None, in_=in_.ap,
            in_offset=bass.IndirectOffsetOnAxis(ap=idx.ap, axis=0), **kw), [in_, idx], [out], is_dma=True,
            cost=1.5, nbytes=nb)

    def _pecost(self, lhsT, rhs):
        n = self._free(rhs.ap)
        m = self._free(lhsT.ap)
        c = (max(n, 48) / 1950.0 + m / 2800.0 + 0.02)
        if lhsT.ap.dtype == F32:
            c *= 4
        return c

    def matmul(self, out, lhsT, rhs, start=True, stop=True):
        reads = [lhsT, rhs] + ([] if start else [out])
        return self.op("pe", lambda e: e.matmul(out.ap, lhsT.ap, rhs.ap, start=start, stop=stop), reads, [out],
                       cost=self._pecost(lhsT, rhs))

    def transpose(self, out, in_, ident):
        return self.op("pe", lambda e: e.transpose(out.ap, in_.ap, ident.ap), [in_, ident], [out],
                       cost=self._pecost(in_, ident))

    def act(self, out, in_, func, bias=None, scale=None, accum_out=None):
        reads = [in_]
        kw = {}
        if bias is not None:
            if isinstance(bias, V):
                reads.append(bias)
                kw["bias"] = bias.ap
            else:
                kw["bias"] = bias
        if scale is not None:
            if isinstance(scale, V):
                reads.append(scale)
                kw["scale"] = scale.ap
            else:
                kw["scale"] = scale
        writes = [out]
        if accum_out is not None:
            writes.append(accum_out)
            kw["accum_out"] = accum_out.ap
        return self.op("act", lambda e: e.activation(out.ap, in_.ap, func, **kw), reads, writes)

    def tt(self, eng, out, in0, in1, op):
        return self.op(eng, lambda e: e.tensor_tensor(out.ap, in0.ap, in1.ap, op), [in0, in1], [out])

    def ts(self, eng, out, in0, s1, op0, s2=None, op1=None):
        reads = [in0]
        a1 = s1.ap if isinstance(s1, V) else s1
        a2 = s2.ap if isinstance(s2, V) else s2
        if isinstance(s1, V):
            reads.append(s1)
        if isinstance(s2, V):
            reads.append(s2)
        if op1 is None:
            return self.op(eng, lambda e: e.tensor_single_scalar(out.ap, in0.ap, a1, op0), reads, [out])
        return self.op(eng, lambda e: e.tensor_scalar(out.ap, in0.ap, a1, a2, op0, op1), reads, [out])

    def stt(self, eng, out, in0, scalar, in1, op0, op1):
        reads = [in0, in1]
        a = scalar.ap if isinstance(scalar, V) else scalar
        if isinstance(scalar, V):
            reads.append(scalar)
        return self.op(eng, lambda e: e.scalar_tensor_tensor(out.ap, in0.ap, a, in1.ap, op0, op1), reads, [out])

    def copy(self, eng, out, in_):
        if eng == "act":
            return self.op("act", lambda e: e.copy(out.ap, in_.ap), [in_], [out])
        return self.op(eng, lambda e: e.tensor_copy(out.ap, in_.ap), [in_], [out])

    def memset(self, eng, out, val):
        return self.op(eng, lambda e: e.memset(out.ap, val), [], [out])

    def recip(self, out, in_):
        return self.op("dve", lambda e: e.reciprocal(out.ap, in_.ap), [in_], [out])

    def barrier(self, markers):
        marks = {}
        for e in ("pe", "act", "dve", "pool"):
            ins = markers[e]()
            assert self.segs[-1][e][-1] is ins
            self.segs[-1][e].pop()
            marks[e] = ins
        outstanding = list(self.dma_last.values())
        tail = {}
        for e in ENGS:
            w = Instr(e, None)
            w.deps = [m for m in marks.values() if m.eng != e] + outstanding
            tail[e] = ([marks[e]] if e in marks else []) + [w]
        self.tails.append(tail)
        self.segs.append({e: [] for e in ENGS})

    def _schedule_segment(self, seg):
        import heapq
        allins = [i for e in ENGS for i in seg[e]]
        inseg = set(id(i) for i in allins)
        for i in allins:
            i.succ = []
            i.nun = 0
            i.avail = 0.0
            i.done = None
        for i in allins:
            for d, kind in i.odeps:
                if id(d) in inseg:
                    d.succ.append(i)
                    i.nun += 1
        heaps = {e: [] for e in ENGS}
        for i in allins:
            if i.nun == 0:
                heapq.heappush(heaps[i.eng], (0.0, i.seq, i))
        free = {e: 0.0 for e in ENGS}
        dma_free = [0.0]
        order = {e: [] for e in ENGS}
        remaining = len(allins)
        while remaining:
            best = None
            for e in ENGS:
                h = heaps[e]
                if not h:
                    continue
                cands = []
                while h and h[0][0] <= free[e] and len(cands) < 64:
                    cands.append(heapq.heappop(h))
                if cands:
                    cands.sort(key=lambda t: t[1])
                    pick = cands[0]
                    for c in cands[1:]:
                        heapq.heappush(h, c)
                    st = free[e]
                else:
                    pick = heapq.heappop(h)
                    st = pick[0]
                if best is None or st < best[0]:
                    if best is not None:
                        heapq.heappush(heaps[best[2].eng], best[1])
                    best = (st, pick, pick[2])
                else:
                    heapq.heappush(h, pick)
            st, pick, ins = best
            e = ins.eng
            if ins.is_dma:
                free[e] = st + ins.cost
                t0 = max(st + ins.cost, dma_free[0])
                dma_free[0] = t0 + ins.nbytes / 150e3
                ins.done = dma_free[0] + 2.0
            elif e == "pe":
                free[e] = st + ins.cost
                ins.done = st + ins.cost + 0.2
            else:
                free[e] = st + ins.cost
                ins.done = st + ins.cost + 0.05
            order[e].append(ins)
            remaining -= 1
            for sc in ins.succ:
                sc.nun -= 1
                if ins.done > sc.avail:
                    sc.avail = ins.done
                if sc.nun == 0:
                    heapq.heappush(heaps[sc.eng], (sc.avail + (0.1 if sc.eng != e else 0.0), sc.seq, sc))
        return order

    def _finalize(self, reorder=True):
        self.streams = {e: [] for e in ENGS}
        for k, seg in enumerate(self.segs):
            order = self._schedule_segment(seg) if reorder else seg
            for e in ENGS:
                self.streams[e].extend(order[e])
                if k < len(self.tails):
                    self.streams[e].extend(self.tails[k][e])
        for e in ENGS:
            for p, ins in enumerate(self.streams[e]):
                ins.pos = p
        for e in ENGS:
            for ins in self.streams[e]:
                if ins.fn is None:
                    for d in ins.deps:
                        d.signal = True
                    continue
                if not ins.odeps and not ins.deps:
                    continue
                best = {}
                deps = list(ins.deps)
                for d, kind in ins.odeps:
                    if d.is_dma:
                        deps.append(d)
                        continue
                    if d.eng == e and not ins.is_dma:
                        assert d.pos < ins.pos
                        if e == "pe" or kind == "war":
                            continue
                    b = best.get(d.eng)
                    if b is None or d.pos > b.pos:
                        best[d.eng] = d
                for d in best.values():
                    d.signal = True
                    deps.append(d)
                ins.deps = deps

    def emit(self, reorder=True):
        nc = self.nc
        self._finalize(reorder)
        for e in ENGS:
            t = 0
            for ins in self.streams[e]:
                if ins.signal and not ins.is_dma and ins.fn is not None:
                    t += 1
                    ins.ticket = t
        with contextlib.ExitStack() as st:
            esem = {e: st.enter_context(nc.semaphore("sem_" + e)) for e in ENGS}
            dsem = {}
            for key in self.dma_count:
                dsem[key] = st.enter_context(nc.semaphore("dsem_%s_%d" % key))
            block = st.enter_context(nc.Block())

            def target(d):
                if d.is_dma:
                    return dsem[d.dma_sem], d.dma_val
                return esem[d.eng], d.ticket

            def run(eng_name, eng):
                waited = {}
                for ins in self.streams[eng_name]:
                    for d in ins.deps:
                        sem, val = target(d)
                        k = id(sem)
                        if waited.get(k, 0) >= val:
                            continue
                        waited[k] = val
                        eng.wait_ge(sem, val)
                    if ins.fn is None:
                        continue
                    r = ins.fn(eng)
                    if LISTING is not None:
                        LISTING.append(str(getattr(r.ins, "name", "?")) + " " + r.concise())
                    if ins.is_dma:
                        r.then_inc(dsem[ins.dma_sem], 16)
                    elif ins.signal:
                        r.then_inc(esem[eng_name], 1)
                if eng_name == "sp":
                    for key, cnt in self.dma_count.items():
                        if waited.get(id(dsem[key]), 0) < 16 * cnt:
                            eng.wait_ge(dsem[key], 16 * cnt)

            @block.tensor
            def _(pe):
                run("pe", pe)

            @block.scalar
            def _(act):
                run("act", act)

            @block.vector
            def _(dve):
                run("dve", dve)

            @block.gpsimd
            def _(pool):
                run("pool", pool)

            @block.sync
            def _(sp):
                run("sp", sp)
        self.stack.close()


_SZ = {F32: 4, BF16: 2, I32: 4}


class Arena:
    def __init__(self, P, nf32):
        self.h = P.stack.enter_context(P.nc.sbuf_tensor("arena", [128, nf32], F32))
        self.n = nf32
        self.off = 0
        self.cnt = 0

    def alloc(self, name, shape, dtype):
        parts = shape[0]
        free = 1
        for s in shape[1:]:
            free *= s
        nbytes = free * _SZ[dtype]
        nf = (nbytes + 31) // 32 * 8
        assert self.off + nf <= self.n, "arena overflow at %s: need %d have %d" % (name, nf, self.n - self.off)
        ap = self.h[0:parts, self.off:self.off + nf]
        self.off += nf
        if dtype != F32:
            ap = ap.bitcast(dtype)
        ap = ap[:, 0:free]
        if len(shape) == 3:
            ap = ap.rearrange("p (a b) -> p a b", a=shape[1])
        elif len(shape) == 4:
            ap = ap.rearrange("p (a b c) -> p a b c", a=shape[1], b=shape[2])
        self.cnt += 1
        return T("%s_%d" % (name, self.cnt), ap)

    def mark(self):
        return self.off

    def reset(self, m):
        self.off = m


class PSBank:
    def __init__(self, ps, b0, nb):
        self.ps = ps
        self.b0 = b0
        self.nb = nb

    def _bufs(self, c0, c1):
        return tuple(self.ps.bufs[self.b0 + b] for b in range(c0 // 512, (c1 - 1) // 512 + 1))

    def f(self, c0, c1, p0=0, p1=128):
        base = self.b0 * 512
        return V(self._bufs(c0, c1), self.ps.h[p0:p1, base + c0:base + c1])

    def bf(self, c0, c1, p0=0, p1=128):
        base = self.b0 * 1024
        return V(self._bufs(c0 // 2, (c1 + 1) // 2), self.ps.hb[p0:p1, base + c0:base + c1])


class PSum:
    def __init__(self, P):
        self.h = P.stack.enter_context(P.nc.psum_tensor("psum_all", [128, 4096], F32))
        self.hb = self.h[:, :].bitcast(BF16)
        self.bufs = [Buf("psb%d" % i) for i in range(8)]
        self.cur = 0
        self.held = set()

    def get(self, nb=1, hold=False):
        c = self.cur
        for _ in range(16):
            if c + nb > 8:
                c = 0
            if all((c + i) not in self.held for i in range(nb)):
                break
            c = (c + 1) % 8
        else:
            raise RuntimeError("no free PSUM banks")
        b = PSBank(self, c, nb)
        self.cur = (c + nb) % 8
        if hold:
            for i in range(nb):
                self.held.add(c + i)
        return b

    def release(self, b):
        for i in range(b.nb):
            self.held.discard(b.b0 + i)


C_IDENT, C_CAUSAL, C_SBLK, C_OBLK8, C_TRIP, C_ONESP, C_SEG, C_CHK, C_ONE = 0, 128, 256, 384, 512, 640, 768, 784, 786
NCST = 800


def make_consts():
    c = np.zeros((128, NCST), np.float32)
    i = np.arange(128)
    s, q = i[:, None], i[None, :]
    c[:, C_IDENT:C_IDENT + 128] = np.eye(128)
    c[:, C_CAUSAL:C_CAUSAL + 128] = (s <= q)
    c[:, C_SBLK:C_SBLK + 128] = (s <= q) & (s // 8 == q // 8)
    c[:, C_OBLK8:C_OBLK8 + 128] = (s // 8 == q // 8)
    c[:, C_TRIP:C_TRIP + 128] = (s <= q) & (s // 64 == q // 64)
    c[:, C_ONESP:C_ONESP + 128] = (s // 64 == q // 64)
    c[:, C_SEG:C_SEG + 16] = (i[:, None] // 8 == np.arange(16)[None, :])
    c[:, C_CHK:C_CHK + 2] = (i[:, None] // 64 == np.arange(2)[None, :])
    c[:, C_ONE:C_ONE + 14] = 1.0
    return c


def rope_tables(seq, past):
    inv = (ROPE_THETA ** (-np.arange(0, 32, 2, dtype=np.float32) / np.float32(32))).astype(np.float32)
    pos_p = np.arange(seq, dtype=np.float32)
    pos_s = (past + np.arange(8)).astype(np.float32)
    pos_s_tile = np.tile(pos_s, 16)
    pos = np.concatenate([pos_p, pos_s_tile])
    ang = (pos[:, None] * inv[None, :]).astype(np.float32)
    cos = np.cos(ang).astype(np.float32)
    sin = np.sin(ang).astype(np.float32)
    nt = seq // 128 + 1
    tok = np.concatenate([cos, sin], axis=1).reshape(nt, 128, 32).transpose(1, 0, 2)
    ct = np.concatenate([cos, cos], axis=1).T
    sg = np.concatenate([-sin, sin], axis=1).T
    feat = np.stack([ct, sg], axis=1)
    return np.ascontiguousarray(tok, np.float32), np.ascontiguousarray(feat, np.float32)


def build_program(cfg):
    NPS, SEQ, NSS, NPG, NPHYS, DFF = cfg["NPS"], cfg["SEQ"], cfg["NSS"], cfg["NPG"], cfg["NPHYS"], cfg["DFF"]
    assert NSS == 16 and SEQ % 128 == 0 and DFF % 128 == 0 and NPG <= 128
    NT = SEQ // 128
    NF = DFF // 128
    NB = NPS + NSS
    TB = min(4, NT)
    NTP = NPS * NT
    NTT = NTP + 1
    TOKP = NPS * SEQ
    KP = NPG
    D = 1024

    nc = bass.Bass("TRN2", target_bir_lowering=False)
    P = Prog(nc)
    din = lambda n, s, d=F32: P.dram(n, s, d, "ExternalInput")
    dout = lambda n, s: P.dram(n, s, F32, "ExternalOutput")
    xp_d = din("xp", [TOKP, D]); xs_d = din("xs", [128, D])
    ckvc_d = din("cache_ckv", [NPHYS, PAGE * 256]); krc_d = din("cache_krope", [NPHYS, PAGE * 32])
    state_d = din("state", [NSS, 4, 64, 128])
    ptT_d = din("ptT", [NPG, NSS], I32)
    cT_d = din("cT", [D, NB])
    wada_d = din("w_ada", [D, 9 * D]); badaT_d = din("b_adaT", [128, 72])
    normsT_d = din("normsT", [128, 24])
    nfin_d = din("nfin_bc", [128, D])
    ffn_d = [[din("f%d_w1" % i, [D, DFF]), din("f%d_w3" % i, [D, DFF]), din("f%d_w2" % i, [DFF, D])] for i in (1, 2)]
    win_d = din("w_in", [D, 2224])
    gqab_d = din("g_qa_bc", [128, 384])
    wqbn_d = din("wqb_n", [384, 512]); wqbr_d = din("wqb_r", [384, 256]); wqbs_d = din("wqb_s", [384, 256])
    gkva_d = din("g_kva_bc", [128, 256])
    wkvbn_d = din("wkvb_nT", [64, 8 * 256]); wkvbv_d = din("wkvb_v", [256, 8 * 64])
    wgate_d = din("wgate", [17, 256])
    ggla_d = din("g_gla_bc", [128, 128])
    wo_d = din("w_o", [D, D])
    cst_d = din("consts", [128, NCST])
    ropeT_d = din("rope_tok", [128, NT + 1, 32]); ropeF_d = din("rope_feat", [32, 2, SEQ + 128])

    yp_d = dout("y_p", [TOKP, D]); ys_d = dout("y_s", [128, D])
    ckvp_d = dout("ckv_p", [TOKP, 256]); krp_d = dout("kr_p", [TOKP, 32]); glap_d = dout("gla_p", [NPS, 4, 64, 128])
    ckvs_d = dout("ckv_s", [128, 256]); krs_d = dout("kr_s", [128, 32]); glas_d = dout("gla_s", [NSS, 4, 64, 128])
    dbg = cfg.get("debug", False)
    phases = cfg.get("phases", ("ffn1", "mixer", "ffn2"))
    X1 = P.dram("x1_scr", [NTT * 128, D], F32, "ExternalOutput" if dbg else "Internal")
    X2 = P.dram("x2_scr", [NTT * 128, D], F32, "ExternalOutput" if dbg else "Internal")

    MIXD = P.dram("mix_dbg", [NTT * 128, D], F32, "ExternalOutput") if dbg else None
    AR = Arena(P, 53100)
    PS = PSum(P)

    cst = AR.alloc("cst", [128, NCST], F32)
    cstb = AR.alloc("cstb", [128, NCST], BF16)
    P.dma("sp", cst[:, :], cst_d[:, :])
    P.dma("pool", cstb[:, :], cst_d[:, :])
    ident_f = lambda n=128: cst[0:n, C_IDENT:C_IDENT + n]
    ident_b = lambda n=128: cstb[0:n, C_IDENT:C_IDENT + n]
    mT = AR.alloc("mT", [128, 72, NB], F32)
    Amod = AR.alloc("Amod", [128, 3, 8, NB], F32)
    normsT = AR.alloc("normsT", [128, 24], F32)
    badaT = AR.alloc("badaT", [128, 72], F32)
    mk = AR.alloc("mk", [128, 8], F32)
    P.dma("sp", normsT[:, :], normsT_d[:, :])
    P.dma("sp", badaT[:, :], badaT_d[:, :])

    def do_barrier():
        def m_pe():
            b = PS.get(1)
            return P.matmul(b.f(0, 1, 0, 1), cstb[0:1, C_ONE:C_ONE + 1], cstb[0:1, C_ONE:C_ONE + 1])
        P.barrier({
            "pe": m_pe,
            "act": lambda: P.copy("act", mk[0:1, 0:1], cst[0:1, C_ONE:C_ONE + 1]),
            "dve": lambda: P.memset("dve", mk[0:1, 2:3], 0.0),
            "pool": lambda: P.memset("pool", mk[0:1, 4:5], 0.0),
        })

    m0 = AR.mark()
    cTs = AR.alloc("cTs", [128, 8, NB], F32)
    scb = AR.alloc("scb", [128, 8, NB], BF16)
    P.dma("sp", cTs[:, :, :], cT_d.v(cT_d.h.rearrange("(k p) b -> p k b", p=128)))
    P.act(scb[:, :, :], cTs[:, :, :], AF.Silu)
    wa = [AR.alloc("wa", [128, 8, 128], BF16) for _ in range(4)]
    wacnt = [0]

    def ada_vectors(vs):
        for v in vs:
            for j in range(8):
                w = wa[wacnt[0] % 4]; wacnt[0] += 1
                c0 = (v * 8 + j) * 128
                P.dma("pool", w[:, :, :], wada_d.v(wada_d.h[:, c0:c0 + 128].rearrange("(k p) c -> p k c", p=128)))
                pb = PS.get(1)
                for k in range(8):
                    P.matmul(pb.f(0, NB), w[:, k, :], scb[:, k, :], start=(k == 0), stop=(k == 7))
                P.act(mT[:, v * 8 + j, :], pb.f(0, NB), AF.Identity, bias=badaT[:, v * 8 + j:v * 8 + j + 1], scale=1.0)

    def ada_mod(k):
        sc = mT[:, (3 * k + 1) * 8:(3 * k + 2) * 8, :]
        P.ts("dve", Amod[:, k, :, :], sc, 1.0, ALU.add)
        nb_ap = normsT.h[:, k * 8:(k + 1) * 8].unsqueeze(2).to_broadcast([128, 8, NB])
        P.tt("dve", Amod[:, k, :, :], Amod[:, k, :, :], normsT.v(nb_ap), ALU.mult)

    ada_vectors([0, 1, 2])
    ada_mod(0)
    SHv = lambda k, kc, c0, c1: mT[:, 3 * k * 8 + kc, c0:c1]
    GTv = lambda k, kc, c0, c1: mT[:, (3 * k + 2) * 8 + kc, c0:c1]

    def tile_info(ti):
        if ti < NTP:
            return ti // NT, ti % NT, False
        return None, NT, True

    def rstd_of(src, n, ss, rs, junk):
        P.act(junk, src, AF.Square, accum_out=ss)
        P.act(rs, ss, AF.Sqrt, scale=1.0 / n, bias=EPS)
        P.recip(rs, rs)

    def norm_mod(xt, k, ti, hT_out, wk):
        s, it, smp = tile_info(ti)
        rstd_of(xt[:, :], D, wk["ss"][:, 0:1], wk["rs"][:, 0:1], wk["junk"][:, :])
        P.ts("dve", wk["xn"][:, :], xt[:, :], wk["rs"][:, 0:1], ALU.mult)
        pt = PS.get(1)
        for kc in range(8):
            P.transpose(pt.bf(kc * 128, (kc + 1) * 128), wk["xn"][:, kc * 128:(kc + 1) * 128], ident_b())
        for kc in range(8):
            src = pt.bf(kc * 128, (kc + 1) * 128)
            if not smp:
                P.act(hT_out(kc), src, AF.Identity, bias=SHv(k, kc, s, s + 1), scale=Amod[:, k, kc, s:s + 1])
            else:
                tmp = wk["mtmp"]
                a_bc = Amod.v(Amod.h[:, k, kc, NPS:NB].unsqueeze(2).to_broadcast([128, 16, 8]))
                s_bc = mT.v(mT.h[:, 3 * k * 8 + kc, NPS:NB].unsqueeze(2).to_broadcast([128, 16, 8]))
                sv = V(src.bufs, src.ap.rearrange("p (b t) -> p b t", b=16))
                tv = tmp.v(tmp.h[:, :].rearrange("p (b t) -> p b t", b=16))
                o = hT_out(kc)
                ov = V(o.bufs, o.ap.rearrange("p (b t) -> p b t", b=16))
                P.tt("dve", tv, sv, a_bc, ALU.mult)
                P.tt("pool", ov, tv, s_bc, ALU.add)

    def build_gate(gate, k, ti, scale, wk):
        s, it, smp = tile_info(ti)
        for kc in range(8):
            tmp = wk["gtmp"]
            if not smp:
                src = mT.v(mT.h[:, (3 * k + 2) * 8 + kc, s:s + 1].to_broadcast([128, 128]))
                P.copy("dve", tmp[:, :], src)
            else:
                src = mT.v(mT.h[:, (3 * k + 2) * 8 + kc, NPS:NB].unsqueeze(2).to_broadcast([128, 16, 8]))
                P.copy("dve", tmp.v(tmp.h[:, :].rearrange("p (b t) -> p b t", b=16)), src)
            pb = PS.get(1)
            P.transpose(pb.f(0, 128), tmp[:, :], ident_f())
            P.act(gate[:, kc * 128:(kc + 1) * 128], pb.f(0, 128), AF.Copy, scale=scale)

    def x_src(ti):
        return xp_d[ti * 128:(ti + 1) * 128, :] if ti < NTP else xs_d[:, :]

    def rows(t, ti):
        return t[ti * 128:(ti + 1) * 128, :]

    def ffn_phase(k, wd, src_fn, dst_fn, final, tail=None, reset_to=None):
        m = AR.mark()
        GW = 4
        NG = (NF + GW - 1) // GW
        gws = [min(GW, NF - g * GW) for g in range(NG)]
        w1g = [AR.alloc("w1g", [128, 8, gws[g] * 128], BF16) for g in range(NG)]
        w3g = [AR.alloc("w3g", [128, 8, gws[g] * 128], BF16) for g in range(NG)]
        w2g = [AR.alloc("w2g", [128, gws[g], D], BF16) for g in range(NG)]
        for g in range(NG):
            c0, c1 = g * GW * 128, (g * GW + gws[g]) * 128
            for kk in range(8):
                P.dma("pool", w1g[g][:, kk, :], wd[0][kk * 128:(kk + 1) * 128, c0:c1])
                P.dma("pool", w3g[g][:, kk, :], wd[1][kk * 128:(kk + 1) * 128, c0:c1])
        for f in range(NF):
            P.dma("pool", w2g[f // GW][:, f % GW, :], wd[2][f * 128:(f + 1) * 128, :])
        w1v = lambda kk, f: w1g[f // GW][:, kk, (f % GW) * 128:(f % GW + 1) * 128]
        w3v = lambda kk, f: w3g[f // GW][:, kk, (f % GW) * 128:(f % GW + 1) * 128]
        w2v = lambda f, c0, c1: w2g[f // GW][:, f % GW, c0:c1]
        xr = [AR.alloc("xr", [128, D], F32) for _ in range(2)]
        xn_ = AR.alloc("xn", [128, D], BF16)
        wk = dict(ss=AR.alloc("ss", [128, 1], F32), rs=AR.alloc("rs", [128, 1], F32),
                  junk=xn_, xn=xn_,
                  mtmp=AR.alloc("mtmp", [128, 128], F32), gtmp=AR.alloc("gtmp", [128, 128], F32))
        nfin = None
        if final:
            nfin = AR.alloc("nfin", [128, D], F32)
            P.dma("sp", nfin[:, :], nfin_d[:, :])
        hT = AR.alloc("hT", [128, 8, TB * 128], BF16)
        gT = AR.alloc("gT", [128, NF, TB * 128], BF16)
        gate = AR.alloc("gate", [128, D], F32)
        sil = [AR.alloc("sil", [128, TB * 128], BF16) for _ in range(2)]
        tmpo = AR.alloc("tmpo", [128, 512], F32)
        blocks = []
        for s in range(NPS):
            for b0 in range(0, NT, TB):
                blocks.append([s * NT + b0 + j for j in range(TB)])
        blocks.append([NTP])
        cur_gate = None
        cnt = 0
        for tiles in blocks:
            N = 128 * len(tiles)
            gkey = tile_info(tiles[0])[0]
            if cur_gate is None or gkey != cur_gate[0]:
                build_gate(gate, k, tiles[0], 0.5, wk)
                cur_gate = (gkey,)
            for j, ti in enumerate(tiles):
                xt = xr[cnt % 2]; cnt += 1
                P.dma("sp", xt[:, :], src_fn(ti))
                norm_mod(xt, k, ti, lambda kc, j=j: hT[:, kc, j * 128:(j + 1) * 128], wk)
            for f in range(NF):
                p1 = PS.get(1); p3 = PS.get(1)
                for kk in range(8):
                    P.matmul(p1.f(0, N), w1v(kk, f), hT[:, kk, 0:N], start=(kk == 0), stop=(kk == 7))
                for kk in range(8):
                    P.matmul(p3.f(0, N), w3v(kk, f), hT[:, kk, 0:N], start=(kk == 0), stop=(kk == 7))
                sl = sil[f % 2]
                P.act(sl[:, 0:N], p1.f(0, N), AF.Silu)
                P.tt("dve", gT[:, f, 0:N], sl[:, 0:N], p3.f(0, N), ALU.mult)
            for j, ti in enumerate(tiles):
                xt = xr[cnt % 2]; cnt += 1
                P.dma("sp", xt[:, :], src_fn(ti))
                o = xt
                for half in range(2):
                    po = PS.get(1)
                    for f in range(NF):
                        P.matmul(po.f(0, 512), gT[:, f, j * 128:(j + 1) * 128], w2v(f, half * 512, (half + 1) * 512),
                                 start=(f == 0), stop=(f == NF - 1))
                    tm = tmpo
                    P.tt("dve", tm[:, :], po.f(0, 512), gate[:, half * 512:(half + 1) * 512], ALU.mult)
                    P.tt("pool", o[:, half * 512:(half + 1) * 512], tm[:, :], xt[:, half * 512:(half + 1) * 512], ALU.add)
                if final:
                    rstd_of(o[:, :], D, wk["ss"][:, 0:1], wk["rs"][:, 0:1], wk["junk"][:, :])
                    P.stt("dve", o[:, :], o[:, :], wk["rs"][:, 0:1], nfin[:, :], ALU.mult, ALU.mult)
                P.dma("sp", dst_fn(ti), o[:, :])
        if tail is not None:
            tail()
        do_barrier()
        AR.reset(m if reset_to is None else reset_to)

    class _Stop(Exception):
        pass

    def chk(level):
        if cfg.get("mix_stop", 99) == level:
            raise _Stop()

    def mixer_phase():
        m = AR.mark()
        try:
            mixer_body()
        except _Stop:
            pass
        do_barrier()
        AR.reset(m)

    def mixer_body():
        A = AR.alloc
        w_in = A("w_in", [128, 8, 2224], BF16)
        w_o = A("w_o", [128, 8, D], BF16)
        for kk in range(8):
            P.dma("pool", w_in[:, kk, :], win_d[kk * 128:(kk + 1) * 128, :])
            P.dma("pool", w_o[:, kk, :], wo_d[kk * 128:(kk + 1) * 128, :])
        gqab = A("gqab", [128, 384], F32)
        P.dma("sp", gqab[:, :], gqab_d[:, :])
        wqn = A("wqn", [128, 3, 512], BF16); wqr = A("wqr", [128, 3, 256], BF16); wqs = A("wqs", [128, 3, 256], BF16)
        for j in range(3):
            P.dma("pool", wqn[:, j, :], wqbn_d[j * 128:(j + 1) * 128, :])
            P.dma("pool", wqr[:, j, :], wqbr_d[j * 128:(j + 1) * 128, :])
            P.dma("pool", wqs[:, j, :], wqbs_d[j * 128:(j + 1) * 128, :])
        wkn = A("wkn", [64, 8, 256], BF16)
        P.dma("pool", wkn.v(wkn.h[:, :, :].rearrange("p h c -> p (h c)")), wkvbn_d[:, :])
        wkv = A("wkv", [128, 2, 8, 64], BF16)
        for cc in range(2):
            P.dma("pool", wkv.v(wkv.h[:, cc, :, :].rearrange("p h v -> p (h v)")), wkvbv_d[cc * 128:(cc + 1) * 128, :])
        wgate = A("wgate", [17, 256], F32)
        P.dma("sp", wgate[:, :], wgate_d[:, :])
        gkva = A("gkva", [128, 256], F32); ggla = A("ggla", [128, 128], F32)
        P.dma("sp", gkva[:, :], gkva_d[:, :]); P.dma("sp", ggla[:, :], ggla_d[:, :])
        ropeT = A("ropeT", [128, NT + 1, 32], F32)
        P.dma("sp", ropeT[:, :, :], ropeT_d[:, :, :])
        ropeFt = [A("ropeFt", [32, 2, 128], F32) for _ in range(2)]

        xr = [A("xr", [128, D], F32) for _ in range(2)]
        xn_ = A("xn", [128, D], BF16)
        wk = dict(ss=A("ss", [128, 1], F32), rs=A("rs", [128, 1], F32), junk=xn_, xn=xn_,
                  mtmp=A("mtmp", [128, 128], F32), gtmp=A("gtmp", [128, 128], F32))
        hT = A("hTm", [128, 8, 128], BF16)
        proj = A("proj", [128, 2224], F32)
        sm = A("sm", [128, 32], F32)
        qan = A("qan", [128, 384], BF16); qanT = A("qanT", [128, 3, 128], BF16)
        qnT = A("qnT", [64, 8, 128], BF16)
        rt1 = A("rt1", [32, 4, 128], F32); rt2 = A("rt2", [32, 4, 128], F32)
        qrT = A("qrT", [32, 8, 128], BF16)
        QlT = A("QlT", [128, 2, 8, 128], BF16)
        ckvf = A("ckvf", [128, 256], F32)
        kro = A("kro", [128, 32], F32); krt = A("krt", [128, 64], F32); krob = A("krob", [128, 32], BF16)
        PTr = [A("PT", [128, 512], BF16) for _ in range(2)]
        Lacc = A("Lacc", [128, 512], F32)
        OT = A("OT", [128, 2, 8, 128], BF16)
        lrow = A("lrow", [1, 1024], F32)
        rl = A("rl", [128, 8], F32)
        mix = A("mix", [128, D], BF16); mixT = A("mixT", [128, 8, 128], BF16)
        gate = A("gate", [128, D], F32)
        tmpo = A("tmpo", [128, 512], F32)
        gaT1 = A("gaT1", [17, 128], F32)
        P.memset("dve", gaT1[:, :], 1.0)
        la = A("la", [128, 256], F32)
        bsb = A("bsb", [128, 256], F32); Ee = A("Ee", [128, 256], F32); Ei = A("Ei", [128, 256], F32); Dd = A("Dd", [128, 256], F32)
        e1 = Dd
        qe = A("qe", [128, 256], BF16); ke = A("ke", [128, 256], BF16); kd = A("kd", [128, 256], BF16)
        vb = A("vb", [128, 512], BF16)
        qeT = A("qeT", [128, 2, 128], BF16); keT = A("keT", [128, 2, 128], BF16)
        AmT = A("AmT", [128, 4, 128], BF16)
        osb = A("osb", [128, 512], F32); sg = A("sg", [128, 512], F32)
        ot = osb
        m_prompt = AR.mark()
        KT = A("KT", [128, 2, SEQ], BF16); KrT = A("KrT", [32, SEQ], BF16); Vq = A("Vq", [128, NT, 256], BF16)
        qe0T = A("qe0T", [128, 2, 128], BF16); qe1T = A("qe1T", [128, 2, 128], BF16)
        P.memset("pool", qe0T[:, :, :], 0.0); P.memset("pool", qe1T[:, :, :], 0.0)
        dec = A("dec", [128, 4], F32)
        S = A("S", [128, 2, 128], F32); S0b = A("S0b", [128, 2, 128], BF16); S1b = A("S1b", [128, 2, 128], BF16)
        ropecnt = [0]

        def mla_q_and_kv(ti, Vdst, KTdst, KrTdst, ckv_out, kr_out):
            s, it, smp = tile_info(ti)
            tok0 = SEQ if smp else it * 128
            rf = ropeFt[ropecnt[0] % 2]; ropecnt[0] += 1
            P.dma("sp", rf[:, :, :], ropeF_d[:, :, tok0:tok0 + 128])
            rstd_of(proj[:, 0:384], 384, sm[:, 0:1], sm[:, 1:2], wk["junk"][:, 0:384])
            P.stt("dve", qan[:, :], proj[:, 0:384], sm[:, 1:2], gqab[:, :], ALU.mult, ALU.mult)
            pq = PS.get(1)
            for j in range(3):
                P.transpose(pq.bf(j * 128, (j + 1) * 128), qan[:, j * 128:(j + 1) * 128], ident_b())
            P.copy("act", qanT.v(qanT.h[:, :, :].rearrange("p a b -> p (a b)")), pq.bf(0, 384))
            pn = PS.get(2)
            for h in range(8):
                for j in range(3):
                    P.matmul(pn.f(h * 128, (h + 1) * 128, 0, 64), wqn[:, j, h * 64:(h + 1) * 64], qanT[:, j, :],
                             start=(j == 0), stop=(j == 2))
            for b in range(2):
                P.copy("act", qnT.v(qnT.h[:, b * 4:(b + 1) * 4, :].rearrange("p a b -> p (a b)")),
                       pn.f(b * 512, (b + 1) * 512, 0, 64))
            pr = PS.get(2)
            for h in range(8):
                for j in range(3):
                    P.matmul(pr.f(h * 128, (h + 1) * 128, 0, 32), wqr[:, j, h * 32:(h + 1) * 32], qanT[:, j, :],
                             start=(j == 0), stop=(j == 2))
            pw = PS.get(2)
            for h in range(8):
                for j in range(3):
                    P.matmul(pw.f(h * 128, (h + 1) * 128, 0, 32), wqs[:, j, h * 32:(h + 1) * 32], qanT[:, j, :],
                             start=(j == 0), stop=(j == 2))
            ct_bc = rf.v(rf.h[:, 0, :].unsqueeze(1).to_broadcast([32, 4, 128]))
            sg_bc = rf.v(rf.h[:, 1, :].unsqueeze(1).to_broadcast([32, 4, 128]))
            for b in range(2):
                prv = pr.f(b * 512, (b + 1) * 512, 0, 32)
                pwv = pw.f(b * 512, (b + 1) * 512, 0, 32)
                P.tt("dve", rt1[:, :, :], V(prv.bufs, prv.ap.rearrange("p (a b) -> p a b", a=4)), ct_bc, ALU.mult)
                P.tt("dve", rt2[:, :, :], V(pwv.bufs, pwv.ap.rearrange("p (a b) -> p a b", a=4)), sg_bc, ALU.mult)
                P.tt("pool", qrT[:, b * 4:(b + 1) * 4, :], rt1[:, :, :], rt2[:, :, :], ALU.add)
            pl = PS.get(4)
            for cc in range(2):
                for h in range(8):
                    c0 = (cc * 8 + h) * 128
                    P.matmul(pl.f(c0, c0 + 128), wkn[:, h, cc * 128:(cc + 1) * 128], qnT[:, h, :])
            for b in range(4):
                cc, hb = b // 2, (b % 2) * 4
                P.copy("act" if b % 2 == 0 else "dve",
                       QlT.v(QlT.h[:, cc, hb:hb + 4, :].rearrange("p a b -> p (a b)")), pl.f(b * 512, (b + 1) * 512))
            rstd_of(proj[:, 384:640], 256, sm[:, 2:3], sm[:, 3:4], wk["junk"][:, 0:256])
            P.stt("dve", ckvf[:, :], proj[:, 384:640], sm[:, 3:4], gkva[:, :], ALU.mult, ALU.mult)
            P.dma("sp", ckv_out, ckvf[:, :])
            P.copy("act", Vdst, ckvf[:, :])
            cosv = ropeT[:, it, 0:16]; sinv = ropeT[:, it, 16:32]
            x1 = proj[:, 640:656]; x2 = proj[:, 656:672]
            P.tt("pool", krt[:, 0:16], x1, cosv, ALU.mult)
            P.tt("pool", krt[:, 16:32], x2, sinv, ALU.mult)
            P.tt("pool", krt[:, 32:48], x1, sinv, ALU.mult)
            P.tt("pool", krt[:, 48:64], x2, cosv, ALU.mult)
            P.tt("pool", kro[:, 0:16], krt[:, 0:16], krt[:, 16:32], ALU.subtract)
            P.tt("pool", kro[:, 16:32], krt[:, 32:48], krt[:, 48:64], ALU.add)
            P.dma("sp", kr_out, kro[:, :])
            P.copy("pool", krob[:, :], kro[:, :])
            pk = PS.get(1)
            P.transpose(pk.bf(0, 128), V(Vdst.bufs, Vdst.ap[:, 0:128]), ident_b())
            P.transpose(pk.bf(128, 256), V(Vdst.bufs, Vdst.ap[:, 128:256]), ident_b())
            P.transpose(pk.bf(256, 384, 0, 32), krob[:, :], ident_b())
            P.copy("act", KTdst(0), pk.bf(0, 128))
            P.copy("dve", KTdst(1), pk.bf(128, 256))
            P.copy("act", KrTdst, pk.bf(256, 384, 0, 32))

        def attend_blocks(blocks, OTd, ld):
            nblk = len(blocks)
            for g in range(2):
                pO = PS.get(2, hold=True)
                for j, (k0, k1, kr, vfn, mask) in enumerate(blocks):
                    pS = PS.get(1)
                    q0 = QlT.v(QlT.h[:, 0, g * 4:(g + 1) * 4, :].rearrange("p a b -> p (a b)"))
                    q1 = QlT.v(QlT.h[:, 1, g * 4:(g + 1) * 4, :].rearrange("p a b -> p (a b)"))
                    qr = qrT.v(qrT.h[:, g * 4:(g + 1) * 4, :].rearrange("p a b -> p (a b)"))
                    P.matmul(pS.f(0, 512), k0, q0, start=True, stop=False)
                    P.matmul(pS.f(0, 512), k1, q1, start=False, stop=False)
                    P.matmul(pS.f(0, 512), kr, qr, start=False, stop=True)
                    pt = PTr[j % 2]
                    P.act(pt[:, :], pS.f(0, 512), AF.Exp, scale=MLA_SCALE)
                    if mask is not None:
                        mb = V(mask.bufs, mask.ap.unsqueeze(1).to_broadcast([128, 4, 128]))
                        ptv = pt.v(pt.h[:, :].rearrange("p (a b) -> p a b", a=4))
                        P.tt("pool", ptv, ptv, mb, ALU.mult)
                    for cc in range(2):
                        P.matmul(pO.f(cc * 512, (cc + 1) * 512), vfn(cc), pt[:, :], start=(j == 0), stop=(j == nblk - 1))
                    if j == 0:
                        P.copy("dve", Lacc[:, :], pt[:, :])
                    else:
                        P.tt("dve", Lacc[:, :], Lacc[:, :], pt[:, :], ALU.add)
                pL = PS.get(1)
                P.matmul(pL.f(0, 512, 0, 1), cst[:, C_ONE:C_ONE + 1], Lacc[:, :])
                for cc in range(2):
                    P.copy("act" if cc == 0 else "dve",
                           OTd.v(OTd.h[:, cc, g * 4:(g + 1) * 4, :].rearrange("p a b -> p (a b)")),
                           pO.f(cc * 512, (cc + 1) * 512))
                P.copy("act", ld[0:1, g * 512:(g + 1) * 512], pL.f(0, 512, 0, 1))
                PS.release(pO)

        def o_mla(OTs, ls):
            if ls is not None:
                plt = PS.get(1)
                for h in range(8):
                    P.matmul(plt.f(h, h + 1), ls[0:1, h * 128:(h + 1) * 128], cst[0:1, C_ONE:C_ONE + 1])
                P.recip(rl[:, :], plt.f(0, 8))
            po = PS.get(1)
            for h in range(8):
                for cc in range(2):
                    P.matmul(po.f(h * 64, (h + 1) * 64), OTs[:, cc, h, :], wkv[:, cc, h, :], start=(cc == 0), stop=(cc == 1))
            pov = po.f(0, 512)
            if ls is None:
                P.copy("act", mix[:, 0:512], pov)
                return
            P.tt("dve", mix.v(mix.h[:, 0:512].rearrange("p (a b) -> p a b", a=8)),
                 V(pov.bufs, pov.ap.rearrange("p (a b) -> p a b", a=8)),
                 rl.v(rl.h[:, :].unsqueeze(2).to_broadcast([128, 8, 64])), ALU.mult)

        def gla_common(smp):
            tri = cst[:, C_SBLK:C_SBLK + 128] if smp else cst[:, C_TRIP:C_TRIP + 128]
            ones = cst[:, C_OBLK8:C_OBLK8 + 128] if smp else cst[:, C_ONESP:C_ONESP + 128]
            pg = PS.get(1)
            P.transpose(pg.f(0, 128, 0, 16), proj[:, 1696:1712], ident_f())
            P.copy("act", gaT1[0:16, :], pg.f(0, 128, 0, 16))
            px = PS.get(1)
            P.matmul(px.f(0, 256), gaT1[0:17, :], wgate[0:17, :])
            P.act(e1[:, :], px.f(0, 256), AF.Exp, scale=-1.0)
            P.act(la[:, :], e1[:, :], AF.Ln, bias=1.0)
            P.ts("dve", la[:, :], la[:, :], -1.0 / 16.0, ALU.mult)
            chk(401)
            pb = PS.get(1)
            P.matmul(pb.f(0, 256), tri, la[:, :])
            P.matmul(pb.f(256, 512), ones, la[:, :])
            P.copy("act", bsb[:, :], pb.f(0, 256))
            P.act(Ee[:, :], bsb[:, :], AF.Exp)
            P.act(Ei[:, :], bsb[:, :], AF.Exp, scale=-1.0)
            P.tt("dve", Dd[:, :], pb.f(256, 512), bsb[:, :], ALU.subtract)
            P.act(Dd[:, :], Dd[:, :], AF.Exp)
            chk(402)
            P.stt("dve", qe[:, :], proj[:, 672:928], 0.125, Ee[:, :], ALU.mult, ALU.mult)
            P.tt("dve", ke[:, :], proj[:, 928:1184], Ei[:, :], ALU.mult)
            P.tt("dve", kd[:, :], proj[:, 928:1184], Dd[:, :], ALU.mult)
            P.copy("act", vb[:, :], proj[:, 1184:1696])
            chk(403)
            pt2 = PS.get(1)
            for p in range(2):
                P.transpose(pt2.bf(p * 128, (p + 1) * 128), qe[:, p * 128:(p + 1) * 128], ident_b())
                P.transpose(pt2.bf(256 + p * 128, 256 + (p + 1) * 128), ke[:, p * 128:(p + 1) * 128], ident_b())
            chk(4031)
            P.copy("act", qeT.v(qeT.h[:, :, :].rearrange("p a b -> p (a b)")), pt2.bf(0, 256))
            chk(4032)
            P.copy("act", keT.v(keT.h[:, :, :].rearrange("p a b -> p (a b)")), pt2.bf(256, 512))
            chk(404)
            pA = [PS.get(1), PS.get(1)]
            for hh in range(2):
                for p in range(2):
                    P.matmul(pA[hh].f(p * 128, (p + 1) * 128), keT[64 * hh:64 * hh + 64, p, :], qeT[64 * hh:64 * hh + 64, p, :])
            for hh in range(2):
                for p in range(2):
                    P.copy("act", AmT[:, 2 * p + hh, :], pA[hh].f(p * 128, (p + 1) * 128))
            trib = cstb[:, C_SBLK:C_SBLK + 128] if smp else cstb[:, C_TRIP:C_TRIP + 128]
            P.tt("pool", AmT[:, :, :], AmT[:, :, :], V(trib.bufs, trib.ap.unsqueeze(1).to_broadcast([128, 4, 128])), ALU.mult)

        def gla_finish(po2):
            for h in range(4):
                P.copy("act", osb[:, h * 128:(h + 1) * 128], po2[h])
            for h in range(4):
                P.act(wk["junk"][:, 0:128], osb[:, h * 128:(h + 1) * 128], AF.Square, accum_out=sm[:, 8 + h:9 + h])
            P.act(sm[:, 12:16], sm[:, 8:12], AF.Sqrt, scale=1.0 / 128, bias=EPS)
            P.recip(sm[:, 12:16], sm[:, 12:16])
            P.act(sg[:, :], proj[:, 1712:2224], AF.Silu)
            sgv = sg.v(sg.h[:, :].rearrange("p (a b) -> p a b", a=4))
            P.tt("pool", sgv, sgv, ggla.v(ggla.h[:, :].unsqueeze(1).to_broadcast([128, 4, 128])), ALU.mult)
            osv = osb.v(osb.h[:, :].rearrange("p (a b) -> p a b", a=4))
            P.tt("dve", osv, osv, sm.v(sm.h[:, 12:16].unsqueeze(2).to_broadcast([128, 4, 128])), ALU.mult)
            P.tt("pool", mix[:, 512:1024], osb[:, :], sg[:, :], ALU.mult)

        def gla_prompt():
            gla_common(False)
            chk(41)
            P.copy("pool", qe0T[:, :, 0:64], qeT[:, :, 0:64])
            P.copy("pool", qe1T[:, :, 64:128], qeT[:, :, 64:128])
            pd = PS.get(1)
            for p in range(2):
                P.matmul(pd.f(p * 2, p * 2 + 2), la[:, p * 128:(p + 1) * 128], cst[:, C_CHK:C_CHK + 2])
            P.act(dec[:, :], pd.f(0, 4), AF.Exp)
            P.copy("act", S0b[:, :, :], S[:, :, :])
            chk(42)

            def update(c):
                for p in range(2):
                    pu = PS.get(1)
                    P.matmul(pu.f(0, 256), kd[64 * c:64 * c + 64, p * 128:(p + 1) * 128], vb[64 * c:64 * c + 64, p * 256:(p + 1) * 256])
                    P.stt("dve", S[0:64, p, :], S[0:64, p, :], dec[0:64, 2 * p + c:2 * p + c + 1], pu.f(0, 128, 0, 64), ALU.mult, ALU.add)
                    P.stt("dve", S[64:128, p, :], S[64:128, p, :], dec[64:128, 2 * p + c:2 * p + c + 1], pu.f(128, 256, 64, 128), ALU.mult, ALU.add)
            update(0)
            chk(43)
            P.copy("act", S1b[:, :, :], S[:, :, :])
            pob = [PS.get(1, hold=True), PS.get(1, hold=True)]
            po2 = [None] * 4
            for hh in range(2):
                r = slice(64 * hh, 64 * hh + 64)
                for p in range(2):
                    h = 2 * p + hh
                    o_ = pob[hh].f(p * 128, (p + 1) * 128)
                    po2[h] = o_
                    P.matmul(o_, AmT[:, h, :], vb[:, h * 128:(h + 1) * 128], start=True, stop=False)
                    P.matmul(o_, qe0T[r, p, :], S0b[r, p, :], start=False, stop=False)
                    P.matmul(o_, qe1T[r, p, :], S1b[r, p, :], start=False, stop=True)
            chk(44)
            update(1)
            chk(45)
            gla_finish(po2)
            PS.release(pob[0]); PS.release(pob[1])

        def out_proj(ti, xt):
            if dbg:
                P.dma("pool", rows(MIXD, ti), mix[:, :])
            pm = PS.get(1)
            for kc in range(8):
                P.transpose(pm.bf(kc * 128, (kc + 1) * 128), mix[:, kc * 128:(kc + 1) * 128], ident_b())
            P.copy("act", mixT.v(mixT.h[:, :, :].rearrange("p a b -> p (a b)")), pm.bf(0, 1024))
            for half in range(2):
                pw_ = PS.get(1)
                for kc in range(8):
                    P.matmul(pw_.f(0, 512), mixT[:, kc, :], w_o[:, kc, half * 512:(half + 1) * 512], start=(kc == 0), stop=(kc == 7))
                P.tt("dve", tmpo[:, :], pw_.f(0, 512), gate[:, half * 512:(half + 1) * 512], ALU.mult)
                P.tt("pool", xt[:, half * 512:(half + 1) * 512], tmpo[:, :], xt[:, half * 512:(half + 1) * 512], ALU.add)
            P.dma("sp", rows(X2, ti), xt[:, :])

        def in_proj(ti):
            xt = xr[ti % 2]
            P.dma("sp", xt[:, :], rows(X1, ti))
            norm_mod(xt, 1, ti, lambda kc: hT[:, kc, :], wk)
            pp = PS.get(5)
            for cg in range(5):
                c0, c1 = cg * 512, min(2224, cg * 512 + 512)
                for kk in range(8):
                    P.matmul(pp.f(c0, c1), hT[:, kk, :], w_in[:, kk, c0:c1], start=(kk == 0), stop=(kk == 7))
            for cg in range(5):
                c0, c1 = cg * 512, min(2224, cg * 512 + 512)
                P.copy("act" if cg % 2 == 0 else "dve", proj[:, c0:c1], pp.f(c0, c1))
            return xt

        for s in range(NPS):
            P.memset("dve", S[:, :, :], 0.0)
            build_gate(gate, 1, s * NT, 1.0, wk)
            for it in range(NT):
                ti = s * NT + it
                xt = in_proj(ti)
                chk(1)
                mla_q_and_kv(ti, Vq[:, it, :], lambda cc, it=it: KT[:, cc, it * 128:(it + 1) * 128],
                             KrT[:, it * 128:(it + 1) * 128], rows(ckvp_d, ti), rows(krp_d, ti))
                chk(2)
                blocks = []
                for j in range(it + 1):
                    blocks.append((KT[:, 0, j * 128:(j + 1) * 128], KT[:, 1, j * 128:(j + 1) * 128], KrT[:, j * 128:(j + 1) * 128],
                                   (lambda cc, j=j: Vq[:, j, cc * 128:(cc + 1) * 128]),
                                   cstb[:, C_CAUSAL:C_CAUSAL + 128] if j == it else None))
                gla_prompt()
                chk(5)
                attend_blocks(blocks, OT, lrow)
                chk(3)
                o_mla(OT, lrow)
                chk(4)
                out_proj(ti, xt)
            for p in range(2):
                P.dma("sp", glap_d.v(glap_d.h[s, 2 * p:2 * p + 2, :, :].rearrange("h k v -> (h k) v")), S[:, p, :])

        chk(6)
        do_barrier()
        AR.reset(m_prompt)
        ti = NTP
        build_gate(gate, 1, ti, 1.0, wk)
        Vn = A("Vn", [128, 256], BF16); KTn = A("KTn", [128, 2, 128], BF16); KrTn = A("KrTn", [32, 128], BF16)
        ps4 = [A("ps4", [128, 64], F32) for _ in range(2)]
        Ls = A("Ls", [128, 64], F32)
        ptT = A("ptT", [128, NSS], I32)
        P.dma("sp", ptT[0:NPG, :], ptT_d[:, :])
        NTG = 8
        KVg = [A("KVg", [128, NTG, 256], BF16) for _ in range(2)]
        KRg = [A("KRg", [128, NTG, 32], BF16) for _ in range(2)]
        KTc = [A("KTc", [128, 8, 128], BF16) for _ in range(2)]
        KRTc = [A("KRTc", [32, 4, 128], BF16) for _ in range(2)]
        PTs = [A("PTs", [128, 256], BF16) for _ in range(2)]
        S0s = A("S0s", [128, 16, 128], F32); S0sb = A("S0sb", [128, 16, 128], BF16)
        decS = A("decS", [128, 2, 16], F32)
        Vblk = A("Vblk", [128, 8, 256], BF16)
        oTs = A("oTs", [128, 4, 128], F32)
        xt = in_proj(ti)
        mla_q_and_kv(ti, Vn[:, 0:256], lambda cc: KTn[:, cc, :], KrTn[:, :], ckvs_d[:, :], krs_d[:, :])
        chk(7)
        gcnt = 0
        ccnt = 0
        NG = PAGE // NTG
        for b in range(NSS):
            pOb3 = [PS.get(1, hold=True) for _ in range(2)]
            q0 = QlT[:, 0, :, 8 * b:8 * b + 8]; q1 = QlT[:, 1, :, 8 * b:8 * b + 8]; qr = qrT[:, :, 8 * b:8 * b + 8]
            pS = PS.get(1)
            pts = PTs[ccnt % 2]; ccnt += 1
            P.matmul(pS.f(0, 64), KTn[:, 0, :], q0, start=True, stop=False)
            P.matmul(pS.f(0, 64), KTn[:, 1, :], q1, start=False, stop=False)
            P.matmul(pS.f(0, 64), KrTn[:, :], qr, start=False, stop=True)
            P.act(pts[:, 0:64], pS.f(0, 64), AF.Exp, scale=MLA_SCALE)
            pv_ = pts.v(pts.h[:, 0:64].rearrange("p (a b) -> p a b", a=8))
            P.tt("pool", pv_, pv_, cstb.v(cstb.h[:, C_SBLK + 8 * b:C_SBLK + 8 * b + 8].unsqueeze(1).to_broadcast([128, 8, 8])), ALU.mult)
            for cc in range(2):
                P.matmul(pOb3[cc].f(0, 64), Vn[:, cc * 128:(cc + 1) * 128], pts[:, 0:64], start=True, stop=False)
            P.copy("dve", Ls[:, :], pts[:, 0:64])
            for gi in range(NG):
                kv = KVg[gcnt % 2]; krg = KRg[gcnt % 2]; gcnt += 1
                P.idma(kv.v(kv.h[0:KP, :, :].rearrange("p t c -> p (t c)")), ckvc_d[:, :], ptT[0:KP, b:b + 1],
                       element_offset=gi * NTG * 256)
                P.idma(krg.v(krg.h[0:KP, :, :].rearrange("p t c -> p (t c)")), krc_d[:, :], ptT[0:KP, b:b + 1],
                       element_offset=gi * NTG * 32)
                for sb in range(NTG // 4):
                    ktc = KTc[ccnt % 2]; krtc = KRTc[ccnt % 2]; pts = PTs[ccnt % 2]; ccnt += 1
                    pk = PS.get(1); pkr = PS.get(1)
                    for q in range(4):
                        t = sb * 4 + q
                        for cc in range(2):
                            c0 = (q * 2 + cc) * 128
                            P.transpose(pk.bf(c0, c0 + KP), kv[0:KP, t, cc * 128:(cc + 1) * 128], ident_b(KP))
                        P.transpose(pkr.bf(q * 128, q * 128 + KP, 0, 32), krg[0:KP, t, :], ident_b(KP))
                    if KP == 128:
                        P.copy("act", ktc.v(ktc.h[:, :, :].rearrange("p a b -> p (a b)")), pk.bf(0, 1024))
                        P.copy("act", krtc.v(krtc.h[:, :, :].rearrange("p a b -> p (a b)")), pkr.bf(0, 512, 0, 32))
                    else:
                        pkv = pk.bf(0, 1024); pkrv = pkr.bf(0, 512, 0, 32)
                        P.copy("act", ktc[:, :, 0:KP], V(pkv.bufs, pkv.ap.rearrange("p (a b) -> p a b", a=8)[:, :, 0:KP]))
                        P.copy("act", krtc[:, :, 0:KP], V(pkrv.bufs, pkrv.ap.rearrange("p (a b) -> p a b", a=4)[:, :, 0:KP]))
                    pS = PS.get(1)
                    for q in range(4):
                        o = pS.f(q * 64, (q + 1) * 64, 0, KP)
                        P.matmul(o, ktc[:, q * 2, 0:KP], q0, start=True, stop=False)
                        P.matmul(o, ktc[:, q * 2 + 1, 0:KP], q1, start=False, stop=False)
                        P.matmul(o, krtc[:, q, 0:KP], qr, start=False, stop=True)
                    P.act(pts[0:KP, :], pS.f(0, 256, 0, KP), AF.Exp, scale=MLA_SCALE)
                    for q in range(4):
                        t = sb * 4 + q
                        last = (gi == NG - 1) and (sb == NTG // 4 - 1) and (q == 3)
                        for cc in range(2):
                            P.matmul(pOb3[cc].f(0, 64), kv[0:KP, t, cc * 128:(cc + 1) * 128], pts[0:KP, q * 64:(q + 1) * 64],
                                     start=False, stop=last)
                    p4 = ps4[ccnt % 2]
                    P.op("dve", lambda e, p4=p4, pts=pts: e.tensor_reduce(
                        p4.h[0:KP, :], pts.h[0:KP, :].rearrange("p (c q) -> p q c", c=4), AX.X, ALU.add),
                        [pts[0:KP, :]], [p4[0:KP, :]])
                    P.tt("dve", Ls[0:KP, :], Ls[0:KP, :], p4[0:KP, :], ALU.add)
            for cc in range(2):
                pv = pOb3[cc].f(0, 64)
                P.copy("dve" if cc == 0 else "act", OT[:, cc, :, 8 * b:8 * b + 8], V(pv.bufs, pv.ap.rearrange("p (a b) -> p a b", a=8)))
            pLs = PS.get(1)
            P.matmul(pLs.f(0, 64, 0, 1), cst[:, C_ONE:C_ONE + 1], Ls[:, :])
            plv = pLs.f(0, 64, 0, 1)
            P.copy("dve", lrow.v(lrow.h[0:1, :].rearrange("p (a b) -> p a b", a=8)[:, :, 8 * b:8 * b + 8]),
                   V(plv.bufs, plv.ap.rearrange("p (a b) -> p a b", a=8)))
            for pb_ in pOb3:
                PS.release(pb_)
        chk(8)
        o_mla(OT, lrow)
        gla_common(True)
        pd = PS.get(1)
        for p in range(2):
            P.matmul(pd.f(p * 16, p * 16 + 16), la[:, p * 128:(p + 1) * 128], cst[:, C_SEG:C_SEG + 16])
        P.act(decS.v(decS.h[:, :, :].rearrange("p a b -> p (a b)")), pd.f(0, 32), AF.Exp)
        poTb = [PS.get(1, hold=True), PS.get(1, hold=True)]
        poT = lambda h, c0, c1: poTb[h % 2].f((h // 2) * 128 + c0, (h // 2) * 128 + c1)
        for p in range(2):
            for hh in range(2):
                P.dma("sp", S0s[64 * hh:64 * hh + 64, :, :],
                      state_d.v(state_d.h[:, 2 * p + hh, :, :].rearrange("b k v -> k b v")))
            P.copy("act", S0sb.v(S0sb.h[:, :, :].rearrange("p a b -> p (a b)")),
                   S0s.v(S0s.h[:, :, :].rearrange("p a b -> p (a b)")))
            for hh in range(2):
                h = 2 * p + hh
                r = slice(64 * hh, 64 * hh + 64)
                for b in range(NSS):
                    P.matmul(poT(h, 8 * b, 8 * b + 8), S0sb[r, b, :], qeT[r, p, 8 * b:8 * b + 8], start=(b == 0), stop=False)
                P.matmul(poT(h, 0, 128), vb[:, h * 128:(h + 1) * 128], AmT[:, h, :], start=False, stop=True)
            for half in range(2):
                P.tt("dve", Vblk[:, :, :], vb.v(vb.h[:, p * 256:(p + 1) * 256].unsqueeze(1).to_broadcast([128, 8, 256])),
                     cst.v(cst.h[:, C_SEG + 8 * half:C_SEG + 8 * half + 8].unsqueeze(2).to_broadcast([128, 8, 256])), ALU.mult)
                for bb in range(4):
                    pu = PS.get(1)
                    P.matmul(pu.f(0, 512), kd[:, p * 128:(p + 1) * 128], Vblk.v(Vblk.h[:, 2 * bb:2 * bb + 2, :].rearrange("p a b -> p (a b)")))
                    for bl in range(2):
                        b = half * 8 + 2 * bb + bl
                        off = bl * 256
                        P.stt("dve", S0s[0:64, b, :], S0s[0:64, b, :], decS[0:64, p, b:b + 1], pu.f(off, off + 128, 0, 64), ALU.mult, ALU.add)
                        P.stt("dve", S0s[64:128, b, :], S0s[64:128, b, :], decS[64:128, p, b:b + 1], pu.f(off + 128, off + 256, 64, 128), ALU.mult, ALU.add)
            for hh in range(2):
                P.dma("sp", glas_d.v(glas_d.h[:, 2 * p + hh, :, :].rearrange("b k v -> k b v")), S0s[64 * hh:64 * hh + 64, :, :])
        for h in range(4):
            P.copy("act", oTs[:, h, :], poT(h, 0, 128))
        PS.release(poTb[0]); PS.release(poTb[1])
        po2b = PS.get(1)
        for h in range(4):
            P.transpose(po2b.f(h * 128, (h + 1) * 128), oTs[:, h, :], ident_f())
        gla_finish([po2b.f(h * 128, (h + 1) * 128) for h in range(4)])
        out_proj(ti, xt)


    if "ffn1" in phases:
        def ada_rest():
            ada_vectors([3, 4, 5, 6, 7, 8])
            ada_mod(1)
            ada_mod(2)
        ffn_phase(0, ffn_d[0], x_src, lambda ti: rows(X1, ti), False, tail=ada_rest, reset_to=m0)
    if "mixer" in phases:
        mixer_phase()
    if "ffn2" in phases:
        ffn_phase(2, ffn_d[1], lambda ti: rows(X2, ti), lambda ti: (rows(yp_d, ti) if ti < NTP else ys_d[:, :]), True)
    P.emit()
    return nc


def prep_inputs(inp, cfg, n_cores):
    NPS, SEQ, NSS, NPG, NPHYS, DFF = cfg["NPS"], cfg["SEQ"], cfg["NSS"], cfg["NPG"], cfg["NPHYS"], cfg["DFF"]
    f = lambda a: np.ascontiguousarray(np.asarray(a), dtype=np.float32)
    col = lambda v: np.ascontiguousarray(np.asarray(v, np.float32).reshape(-1, 128).T)
    shared = {}
    shared["cache_ckv"] = f(inp["cache_ckv"][0]).reshape(NPHYS, PAGE * 256)
    shared["cache_krope"] = f(inp["cache_krope"][0]).reshape(NPHYS, PAGE * 32)
    shared["w_ada"] = f(inp["w_ada"][0])
    shared["b_adaT"] = col(inp["b_ada"][0])
    shared["normsT"] = np.ascontiguousarray(np.concatenate(
        [col(inp["norm_ffn1"][0]), col(inp["norm_mix"][0]), col(inp["norm_ffn2"][0])], axis=1))
    shared["nfin_bc"] = np.ascontiguousarray(np.broadcast_to(f(inp["norm_final"]).reshape(1, -1), (128, 1024)))
    for i, nm in ((1, "ffn1"), (2, "ffn2")):
        shared["f%d_w1" % i] = f(inp[nm + "_w1"][0]); shared["f%d_w3" % i] = f(inp[nm + "_w3"][0]); shared["f%d_w2" % i] = f(inp[nm + "_w2"][0])
    shared["w_in"] = f(inp["w_in"][0])
    shared["g_qa_bc"] = np.ascontiguousarray(np.broadcast_to(f(inp["g_qa"][0]).reshape(1, -1), (128, 384)))
    wqb = f(inp["w_qb"][0])
    shared["wqb_n"] = np.ascontiguousarray(wqb[:, :, 0:64].reshape(384, 512))
    shared["wqb_r"] = np.ascontiguousarray(wqb[:, :, 64:96].reshape(384, 256))
    shared["wqb_s"] = np.ascontiguousarray(np.concatenate([wqb[:, :, 80:96], wqb[:, :, 64:80]], axis=2).reshape(384, 256))
    shared["g_kva_bc"] = np.ascontiguousarray(np.broadcast_to(f(inp["g_kva"][0]).reshape(1, -1), (128, 256)))
    wkvb = f(inp["w_kvb"][0])
    shared["wkvb_nT"] = np.ascontiguousarray(wkvb[:, :, 0:64].transpose(2, 1, 0).reshape(64, 8 * 256))
    shared["wkvb_v"] = np.ascontiguousarray(wkvb[:, :, 64:128].reshape(256, 8 * 64))
    shared["wgate"] = np.ascontiguousarray(np.concatenate([f(inp["w_gate_b"][0]), f(inp["b_gate"][0]).reshape(1, -1)], axis=0))
    shared["g_gla_bc"] = np.ascontiguousarray(np.broadcast_to(f(inp["g_gla_o"][0]).reshape(1, -1), (128, 128)))
    shared["w_o"] = f(inp["w_o"][0])
    shared["consts"] = make_consts()
    tok, feat = rope_tables(SEQ, NPG * PAGE)
    shared["rope_tok"] = tok
    shared["rope_feat"] = feat
    xp = f(inp["x_prompt"]); xs = f(inp["x_sample"])
    cp = f(inp["c_prompt"]); cs = f(inp["c_sample"])
    st = f(inp["state_gla"][0])
    pt = np.asarray(inp["page_table"]).astype(np.int32)
    maps = []
    for c in range(n_cores):
        d = dict(shared)
        d["xp"] = np.ascontiguousarray(xp[c * NPS:(c + 1) * NPS].reshape(NPS * SEQ, 1024))
        d["xs"] = np.ascontiguousarray(xs[c * NSS:(c + 1) * NSS].reshape(NSS * 8, 1024))
        d["state"] = np.ascontiguousarray(st[c * NSS:(c + 1) * NSS])
        d["ptT"] = np.ascontiguousarray(pt[c * NSS:(c + 1) * NSS].T)
        d["cT"] = np.ascontiguousarray(np.concatenate([cp[c * NPS:(c + 1) * NPS], cs[c * NSS:(c + 1) * NSS]], axis=0).T)
        maps.append(d)
    return maps


def assemble(results, cfg, n_cores):
    NPS, SEQ, NSS = cfg["NPS"], cfg["SEQ"], cfg["NSS"]
    cat = lambda k: np.concatenate([np.asarray(r[k]) for r in results], axis=0)
    y_p = cat("y_p").reshape(n_cores * NPS, SEQ, 1024)
    y_s = cat("y_s").reshape(n_cores * NSS, 8, 1024)
    ckv_p = cat("ckv_p").reshape(1, n_cores * NPS, SEQ, 256)
    kr_p = cat("kr_p").reshape(1, n_cores * NPS, SEQ, 32)
    gla_p = cat("gla_p").reshape(1, n_cores * NPS, 4, 64, 128)
    ckv_s = cat("ckv_s").reshape(1, n_cores * NSS, 8, 256)
    kr_s = cat("kr_s").reshape(1, n_cores * NSS, 8, 32)
    gla_s = cat("gla_s").reshape(1, n_cores * NSS, 4, 64, 128)
    return tuple(np.ascontiguousarray(a, dtype=np.float32) for a in (y_p, y_s, ckv_p, kr_p, gla_p, ckv_s, kr_s, gla_s))


FULL_CFG = dict(NPS=2, SEQ=2048, NSS=16, NPG=128, NPHYS=20480, DFF=2816)


def kernel(**inputs):
    cfg = dict(FULL_CFG)
    n_cores = 8
    nc = build_program(cfg)
    maps = prep_inputs(inputs, cfg, n_cores)
    res = run_bass_kernel_spmd(nc, maps, core_ids=list(range(n_cores)))
    return assemble(res.results, cfg, n_cores)
```
